# Optimizing a Trainium2 kernel written in Bass

```python
import math
import jax, jax.numpy as jnp
from jax import lax
import numpy as np

D_MODEL = 1024
BATCH = 4
SEQ = 8192
DEPTH = 4
DEC_BATCH = 32
DEC_SEQ = 2048
PAST_LEN = 128

HEAD_DIM = 64
HEADS_PER_GROUP = 4
DILATED_GROUPS = ((128, 1), (512, 4), (2048, 16))
N_GROUPS_A = 3
N_HEADS_A = N_GROUPS_A * HEADS_PER_GROUP
ATTN_WIDTH = N_HEADS_A * HEAD_DIM
ATTN_OUT = HEADS_PER_GROUP * HEAD_DIM
NUM_BUCKETS = 32
MAX_DISTANCE = 1024
SG_CHUNK = 128
SG_GROUPS = 8
SG_WIDTH = 1024
SG_GROUP_CH = SG_WIDTH // SG_GROUPS
D_FF = 2816
CONV_WIDTH = 3
IN_WIDTH = 3 * ATTN_WIDTH + 2 * SG_WIDTH + 2 * D_MODEL
EPS = 1e-6

kernel_name = "hybrid_dilated_attn_spatial_gating_encoder"


def rmsnorm(x, g):
    xf = x.astype(jnp.float32)
    y = xf * lax.rsqrt(jnp.mean(xf * xf, axis=-1, keepdims=True) + EPS)
    return (y * g.astype(jnp.float32)).astype(x.dtype)


def t5_bucket(rel):
    nb = NUM_BUCKETS // 2
    max_exact = nb // 2
    ret = np.where(rel > 0, nb, 0)
    n = np.abs(rel)
    nf = np.maximum(n, 1).astype(np.float32)
    large = max_exact + (np.log(nf / max_exact) / math.log(MAX_DISTANCE / max_exact) * (nb - max_exact)).astype(np.int32)
    large = np.minimum(large, nb - 1)
    return (ret + np.where(n < max_exact, n, large)).astype(np.int32)


def dilated_window_attn(q, k, v, bias_tab, dil, half):
    B, S, H, Dh = q.shape
    L = S // dil
    qb_size = half
    nblk = -(-L // qb_size)
    Lp = nblk * qb_size

    def to_sub(t):
        t = t.reshape(B, L, dil, H, Dh).transpose(0, 2, 1, 3, 4)
        return jnp.pad(t, ((0, 0), (0, 0), (0, Lp - L), (0, 0), (0, 0)))

    def neighbourhood(t):
        t = jnp.pad(t, ((0, 0), (0, 0), (qb_size, qb_size), (0, 0), (0, 0)))
        t = t.reshape(B, dil, nblk + 2, qb_size, H, Dh)
        return jnp.concatenate([t[:, :, :-2], t[:, :, 1:-1], t[:, :, 2:]], axis=3)

    qs = to_sub(q).reshape(B, dil, nblk, qb_size, H, Dh)
    kw = neighbourhood(to_sub(k))
    vw = neighbourhood(to_sub(v))

    p_idx = np.arange(qb_size)[:, None]
    c_idx = np.arange(3 * qb_size)[None, :]
    rel_sub = c_idx - qb_size - p_idx
    band = np.abs(rel_sub) <= half
    key_pos = (np.arange(nblk)[:, None] - 1) * qb_size + np.arange(3 * qb_size)[None, :]
    kvalid = (key_pos >= 0) & (key_pos < L)
    mask = jnp.asarray(band[None] & kvalid[:, None, :])
    bucket = jnp.asarray(t5_bucket(rel_sub * dil))
    bias = jnp.transpose(bias_tab.astype(jnp.float32)[bucket], (2, 0, 1))

    scale = Dh ** -0.5
    s = jnp.einsum('bgnqhd,bgnkhd->bgnhqk', qs.astype(jnp.float32), kw.astype(jnp.float32)) * scale + bias
    s = jnp.where(mask[None, None, :, None], s, -1e30)
    m = jnp.max(s, axis=-1, keepdims=True)
    p = jnp.exp(s - m)
    denom = jnp.sum(p, axis=-1, keepdims=True)
    o = jnp.einsum('bgnhqk,bgnkhd->bgnqhd', p / denom, vw.astype(jnp.float32))
    lse = jnp.transpose((m + jnp.log(denom))[..., 0], (0, 1, 2, 4, 3))

    def from_sub(t):
        rest = t.shape[4:]
        t = t.reshape((B, dil, Lp) + rest)[:, :, :L]
        t = jnp.moveaxis(t, 1, 2)
        return t.reshape((B, S) + rest)

    return from_sub(o), from_sub(lse)


def spatial_gating(u, v, v_gain, w_s, b_s):
    B, S, C = v.shape
    vn = rmsnorm(v, v_gain).reshape(B, S // SG_CHUNK, SG_CHUNK, SG_GROUPS, SG_GROUP_CH)
    z = jnp.einsum('gpq,bnqgc->bnpgc', w_s, vn) + b_s.T[:, :, None]
    return u * z.reshape(B, S, C)


def token_mixer(h, rel_bias, w_in, v_gain, w_s, b_s, w_proj_a, w_proj_b, w_out):
    B, S, _ = h.shape
    proj = jnp.einsum('bsd,de->bse', h, w_in)
    splits = [ATTN_WIDTH, 2 * ATTN_WIDTH, 3 * ATTN_WIDTH, 3 * ATTN_WIDTH + 2 * SG_WIDTH]
    q, k, v, uv, gates = jnp.split(proj, splits, axis=-1)
    shp = (B, S, N_GROUPS_A, HEADS_PER_GROUP, HEAD_DIM)
    q, k, v = q.reshape(shp), k.reshape(shp), v.reshape(shp)

    outs, lses = [], []
    for g, (win, dil) in enumerate(DILATED_GROUPS):
        o_g, l_g = dilated_window_attn(q[:, :, g], k[:, :, g], v[:, :, g],
                                       rel_bias[:, g * HEADS_PER_GROUP:(g + 1) * HEADS_PER_GROUP],
                                       dil, win // (2 * dil))
        outs.append(o_g)
        lses.append(l_g)
    wts = jax.nn.softmax(jnp.stack(lses, axis=0), axis=0)
    o_a = jnp.sum(wts[..., None] * jnp.stack(outs, axis=0), axis=0)
    y_a = jnp.einsum('bse,ed->bsd', o_a.reshape(B, S, ATTN_OUT).astype(h.dtype), w_proj_a)

    uv = jax.nn.gelu(uv)
    u, vv = jnp.split(uv, 2, axis=-1)
    y_b = jnp.einsum('bse,ed->bsd', spatial_gating(u, vv, v_gain, w_s, b_s), w_proj_b)

    g_a, g_b = jnp.split(gates, 2, axis=-1)
    merged = jax.nn.sigmoid(g_a) * y_a + jax.nn.sigmoid(g_b) * y_b
    return jnp.einsum('bsd,de->bse', merged, w_out)


def conv_ffn(h, w_up, conv_w, conv_b, w_down):
    a = jnp.einsum('bsd,df->bsf', h, w_up)
    ap = jnp.pad(a, ((0, 0), (1, 1), (0, 0)))
    a = ap[:, :-2] * conv_w[0] + ap[:, 1:-1] * conv_w[1] + ap[:, 2:] * conv_w[2] + conv_b
    gate, val = jnp.split(a, 2, axis=-1)
    return jnp.einsum('bsf,fd->bsd', jax.nn.gelu(gate) * val, w_down)


def trunk(x, rel_bias, norm_mix, w_in, v_gain, w_s, b_s, w_proj_a, w_proj_b, w_out,
          norm_ffn, w_up, conv_w, conv_b, w_down, norm_final):
    for l in range(DEPTH):
        x = x + token_mixer(rmsnorm(x, norm_mix[l]), rel_bias, w_in[l], v_gain[l], w_s[l], b_s[l],
                            w_proj_a[l], w_proj_b[l], w_out[l])
        x = x + conv_ffn(rmsnorm(x, norm_ffn[l]), w_up[l], conv_w[l], conv_b[l], w_down[l])
    return rmsnorm(x, norm_final)


def setup_inputs(seed: int = 0) -> dict:
    key = jax.random.key(seed)
    ks = jax.random.split(key, 20)
    f32 = jnp.float32
    res_scale = (2.0 * DEPTH) ** -0.5
    nrm = lambda k, shp, s: jax.random.normal(k, shp, f32) * s
    return {
        "x_prompt": nrm(ks[0], (BATCH, SEQ, D_MODEL), 1.0),
        "x_sample": nrm(ks[1], (DEC_BATCH, DEC_SEQ, D_MODEL), 1.0),
        "rel_bias": nrm(ks[2], (NUM_BUCKETS, N_HEADS_A), 0.5),
        "norm_mix": 1.0 + nrm(ks[3], (DEPTH, D_MODEL), 0.02),
        "w_in": nrm(ks[4], (DEPTH, D_MODEL, IN_WIDTH), D_MODEL ** -0.5),
        "v_gain": 1.0 + nrm(ks[5], (DEPTH, SG_WIDTH), 0.02),
        "w_s": nrm(ks[6], (DEPTH, SG_GROUPS, SG_CHUNK, SG_CHUNK), SG_CHUNK ** -0.5),
        "b_s": 1.0 + nrm(ks[7], (DEPTH, SG_GROUPS, SG_CHUNK), 0.02),
        "w_proj_a": nrm(ks[8], (DEPTH, ATTN_OUT, D_MODEL), ATTN_OUT ** -0.5),
        "w_proj_b": nrm(ks[9], (DEPTH, SG_WIDTH, D_MODEL), SG_WIDTH ** -0.5),
        "w_out": nrm(ks[10], (DEPTH, D_MODEL, D_MODEL), D_MODEL ** -0.5 * res_scale),
        "norm_ffn": 1.0 + nrm(ks[11], (DEPTH, D_MODEL), 0.02),
        "w_up": nrm(ks[12], (DEPTH, D_MODEL, 2 * D_FF), D_MODEL ** -0.5),
        "conv_w": nrm(ks[13], (DEPTH, CONV_WIDTH, 2 * D_FF), CONV_WIDTH ** -0.5),
        "conv_b": nrm(ks[14], (DEPTH, 2 * D_FF), 0.02),
        "w_down": nrm(ks[15], (DEPTH, D_FF, D_MODEL), D_FF ** -0.5 * res_scale),
        "norm_final": 1.0 + nrm(ks[16], (D_MODEL,), 0.02),
    }


def reference(x_prompt, x_sample, rel_bias, norm_mix, w_in, v_gain, w_s, b_s, w_proj_a, w_proj_b,
              w_out, norm_ffn, w_up, conv_w, conv_b, w_down, norm_final):
    y_prompt = trunk(x_prompt, rel_bias, norm_mix, w_in, v_gain, w_s, b_s, w_proj_a, w_proj_b, w_out,
                     norm_ffn, w_up, conv_w, conv_b, w_down, norm_final)
    y_sample = trunk(x_sample, rel_bias, norm_mix, w_in, v_gain, w_s, b_s, w_proj_a, w_proj_b, w_out,
                     norm_ffn, w_up, conv_w, conv_b, w_down, norm_final)
    return (y_prompt, y_sample)
```

```python
import contextlib
import os
_SKIP = os.environ.get('KSKIP', '')
import math
import numpy as np
import concourse.bass as bass
import concourse.mybir as mybir
from concourse.bass_utils import run_bass_kernel_spmd

F32 = mybir.dt.float32
BF16 = mybir.dt.bfloat16
AF = mybir.ActivationFunctionType
ALU = mybir.AluOpType
AX = mybir.AxisListType

D = 1024
SEG = 2048
PAD = 1024
UNIT = 1024
DIL = (1, 4, 16)
NEG = -30000.0
EPS = 1e-6
DFF = 2816
NPAIR = 22
STQ = "sp"


class Buf:
    __slots__ = ("name", "w", "r", "pr")

    def __init__(self, name="", init=None):
        self.name = name
        self.w = {}
        self.r = dict(init) if init else {}
        self.pr = {}


def _merge(d, k, v):
    if d.get(k, 0) < v:
        d[k] = v


class Sched:
    CH = 8000
    ENGS = ("pe", "act", "dve", "pool", "sp")

    def __init__(self, nc, same_engine_sync=True):
        self.nc = nc
        self.ops = {e: [] for e in self.ENGS}
        self.count = {e: 0 for e in self.ENGS}
        self.waited = {}
        self.maxep = {}
        self.dcum = {}
        self.same = same_engine_sync
        self.nwait = 0
        self.allbufs = []

    def buf(self, name="", init=None):
        b = Buf(name, init)
        self.allbufs.append(b)
        return b

    def fence(self, bufs):
        ev = {}
        for b in bufs:
            for k, v in b.w.items():
                _merge(ev, k, v)
            for k, v in b.r.items():
                _merge(ev, k, v)
            for k, v in b.pr.items():
                _merge(ev, k, v)
        return ev

    def absorb(self, targets, sources):
        ev = self.fence(sources)
        for t in targets:
            for k, v in ev.items():
                _merge(t.r, k, v)

    def _deps(self, eng, reads, writes, nowaw):
        deps = {}
        for b in reads:
            for k, v in b.w.items():
                _merge(deps, k, v)
        for b in writes:
            for k, v in b.r.items():
                _merge(deps, k, v)
            for k, v in b.pr.items():
                _merge(deps, k, v)
            if not nowaw:
                for k, v in b.w.items():
                    _merge(deps, k, v)
        waits = []
        for k, v in deps.items():
            if k[0] == "e":
                if k[1] == eng and (eng == "pe" or not self.same):
                    continue
                if self.maxep.get((eng, k[1]), -1) > k[2]:
                    continue
            if self.waited.get((eng, k), 0) >= v:
                continue
            self.waited[(eng, k)] = v
            if k[0] == "e":
                self.maxep[(eng, k[1])] = max(self.maxep.get((eng, k[1]), -1), k[2])
            waits.append((k, v))
        return waits

    def _update(self, ev, reads, writes, nowaw):
        k, v = ev
        for b in reads:
            _merge(b.r, k, v)
        for b in writes:
            if b.r or not nowaw:
                b.w = {k: v}
                if b.r:
                    b.pr = b.r
                b.r = {}
            else:
                _merge(b.w, k, v)

    def op(self, eng, fn, reads=(), writes=(), nowaw=False):
        waits = self._deps(eng, reads, writes, nowaw)
        n = self.count[eng]
        self.count[eng] = n + 1
        ev = (("e", eng, n // self.CH), n % self.CH + 1)
        self.ops[eng].append((waits, fn, ev[0], 1))
        self._update(ev, reads, writes, nowaw)
        self.nwait += len(waits)

    def dma(self, q, fn, reads=(), writes=(), sem=None, nowaw=False):
        key = ("d", sem)
        waits = self._deps(q, reads, writes, nowaw)
        prev = self.dcum.get(key, 0)
        if prev and self.waited.get((q, key), 0) < prev:
            self.waited[(q, key)] = prev
            waits.append((key, prev))
        cur = prev + 16
        assert cur < 60000, f"dma sem overflow {sem}"
        self.dcum[key] = cur
        self.ops[q].append((waits, fn, key, 16))
        self._update((key, cur), reads, writes, nowaw)
        self.nwait += len(waits)

    def emit(self):
        nc = self.nc
        keys = set()
        for e in self.ENGS:
            for waits, fn, k, inc in self.ops[e]:
                keys.add(k)
        keys = sorted(keys, key=str)
        print(f"[sched] ops={ {e: len(self.ops[e]) for e in self.ENGS} } waits={self.nwait} sems={len(keys)}", flush=True)
        with contextlib.ExitStack() as st:
            sems = {}
            for i, k in enumerate(keys):
                sems[k] = st.enter_context(nc.semaphore(f"s{i}"))
            block = st.enter_context(nc.Block())
            finals = list(self.dcum.items())

            def replay(eng, e):
                for waits, fn, k, inc in self.ops[eng]:
                    for wk, wv in waits:
                        e.wait_ge(sems[wk], wv)
                    fn(e).then_inc(sems[k], inc)
                if eng == "sp":
                    for k, v in finals:
                        e.wait_ge(sems[k], v)

            @block.tensor
            def _(e):
                replay("pe", e)

            @block.scalar
            def _(e):
                replay("act", e)

            @block.vector
            def _(e):
                replay("dve", e)

            @block.gpsimd
            def _(e):
                replay("pool", e)

            @block.sync
            def _(e):
                replay("sp", e)


def _t5_bucket(rel):
    nb = 16
    max_exact = 8
    ret = np.where(rel > 0, nb, 0)
    n = np.abs(rel)
    nf = np.maximum(n, 1).astype(np.float32)
    large = max_exact + (np.log(nf / max_exact) / math.log(1024 / max_exact) * (nb - max_exact)).astype(np.int32)
    large = np.minimum(large, nb - 1)
    return (ret + np.where(n < max_exact, n, large)).astype(np.int32)


def _planes():
    planes = []
    desc = []
    p = np.arange(128)[:, None]
    n = np.arange(128)[None, :]
    for g, d in enumerate(DIL):
        for ty, c0 in enumerate((-64, 64)):
            rel = p - n + c0
            band = np.abs(rel) <= 64
            bk = _t5_bucket(rel * d)
            lst = []
            for b in range(32):
                m = band & (bk == b)
                if m.any():
                    lst.append((b, len(planes)))
                    planes.append(m.astype(np.float32))
            lst.append((-1, len(planes)))
            planes.append((~band).astype(np.float32))
            desc.append(lst)
    return np.stack(planes), desc


_PLANES, _PDESC = _planes()


class _Stop(Exception):
    pass


def build(nseg, depth, debug=False, stop=None):
    try:
        return _build(nseg, depth, debug, stop)
    except _Stop as e:
        return e.args[0]


def _build(nseg, depth, debug=False, stop=None):
    T = nseg * SEG
    NU = T // UNIT
    nc = bass.Bass("TRN2", target_bir_lowering=False)
    S = Sched(nc)

    def din(name, shape, dt=F32):
        return nc.dram_tensor(name, list(shape), dt, kind="ExternalInput").ap()

    x_in = din("x", [T, D])
    segmask_in = din("segmask", [128, 2 * nseg])
    convflag_in = din("convflag", [128, 2 * NU])
    relb_in = din("relb", [1, 384])
    epl_in = din("epl", list(_PLANES.shape))
    ident_in = din("ident", [128, 128])
    w_in = din("w_in", [depth, D, 6400])
    w_sT = din("w_sT", [depth, 128, 1024])
    w_pa = din("w_pa", [depth, 64, 4, 1024])
    w_pb = din("w_pb", [depth, D, D])
    w_o = din("w_o", [depth, D, D])
    w_up = din("w_up", [depth, D, 2 * DFF])
    w_dn = din("w_dn", [depth, DFF, D])
    pp_in = din("pp", [depth, 128, 192])
    nf_in = din("nf", [128, 8])
    vg_in = din("vg", [depth, 1, 1024])
    bs_in = din("bs", [depth, 1, 1024])
    y_out = nc.dram_tensor("y", [T, D], F32, kind="ExternalOutput").ap()

    def dscr(name, shape, dt):
        return nc.dram_tensor(name, list(shape), dt, kind="ExternalOutput" if debug else "Internal").ap()

    XA = dscr("XA", [D, T + 2], F32)
    XB = dscr("XB", [D, T + 2], F32)
    HT = dscr("HT", [D, T], BF16)
    KT = dscr("KT", [768, PAD + T + PAD], BF16)
    VS = dscr("VS", [PAD + T + PAD, 768], BF16)
    XAv = XA.rearrange("(kc p) t -> p kc t", p=128)
    XBv = XB.rearrange("(kc p) t -> p kc t", p=128)
    HTv = HT.rearrange("(kc p) t -> p kc t", p=128)
    XAb = [S.buf(f"XA{i}") for i in range(T // 512)]
    XBb = [S.buf(f"XB{i}") for i in range(T // 512)]
    XAh, XBh = S.buf("XAh"), S.buf("XBh")
    HTb = [S.buf(f"HT{i}") for i in range(nseg)]
    KTb = [S.buf(f"KT{i}") for i in range(nseg)]
    VSb = [S.buf(f"VS{i}") for i in range(nseg)]
    KTp, VSp = S.buf("KTp"), S.buf("VSp")

    TOTAL = 207 * 1024
    M = nc.alloc_sbuf_tensor("M", [128, TOTAL // 2], BF16).ap()

    def view(off, shape, dt, parts=128):
        n = int(np.prod(shape))
        nb = n * (4 if dt == F32 else 2)
        assert off % 4 == 0
        v = M[0:parts, off // 2:(off + nb) // 2]
        if dt == F32:
            v = v.bitcast(F32)
        if len(shape) == 2:
            v = v.rearrange("p (a b) -> p a b", a=shape[0])
        elif len(shape) == 3:
            v = v.rearrange("p (a b c) -> p a b c", a=shape[0], b=shape[1])
        return v

    class Bump:
        def __init__(self, base, limit):
            self.o = base
            self.limit = limit

        def __call__(self, shape, dt, parts=128):
            n = int(np.prod(shape)) * (4 if dt == F32 else 2)
            n = (n + 31) // 32 * 32
            off = self.o
            self.o += n
            assert self.o <= self.limit, f"sbuf overflow {self.o} > {self.limit}"
            return view(off, shape, dt, parts)

    P = Bump(0, 24 * 1024)
    ident_f = P([128], F32)
    ones_mean = P([128], BF16)
    ones_bf = P([128], BF16)
    epscol = P([1], F32)
    zerocol = P([1], F32)
    segmask = P([2 * nseg], F32)
    convflag = P([2 * NU], F32)
    relb = P([384], F32)
    nfg = P([8], F32)
    TAB = [P([8, 128], F32) for g in range(3)]
    pp = P([192], F32)
    vg_bc = P([1024], F32)
    bs_bc = P([8, 128], F32)
    cb = S.buf("const")
    ppb = S.buf("pp")
    tabb = S.buf("tab")
    NS, NB = 2, 5
    RB = Bump(P.limit, P.limit + NS * 8192 + NB * 4096)
    stg = [RB([2048], F32) for i in range(NS)]
    wbf = [RB([2048], BF16) for i in range(NB)]
    stgb = [S.buf(f"stg{i}") for i in range(NS)]
    wbb = [S.buf(f"wb{i}") for i in range(NB)]
    WBASE = RB.limit
    WLIM = TOTAL

    banks = [nc.alloc_psum_tensor(f"bank{i}", [128, 512], F32).ap() for i in range(8)]
    bankb = [S.buf(f"bank{i}") for i in range(8)]
    pbi = [0]

    def pb():
        i = pbi[0] % 8
        pbi[0] += 1
        return banks[i], bankb[i]

    dkc = {}

    def dk(name, n):
        i = dkc.get(name, 0)
        dkc[name] = i + 1
        return f"{name}{i % n}"

    def MM(out, lhsT, rhs, start, stop, reads, writes):
        S.op("pe", lambda e: e.matmul(out, lhsT=lhsT, rhs=rhs, start=start, stop=stop), reads, writes, nowaw=True)

    def TR(out, in_, reads, writes):
        S.op("pe", lambda e: e.transpose(out, in_, ident_f), list(reads) + [cb], writes, nowaw=True)

    def ACT(out, in_, func, reads, writes, bias=None, scale=1.0, nowaw=True):
        if bias is None:
            S.op("act", lambda e: e.activation(out=out, in_=in_, func=func, scale=scale), reads, writes, nowaw=nowaw)
        else:
            S.op("act", lambda e: e.activation(out=out, in_=in_, func=func, bias=bias, scale=scale), reads, writes, nowaw=nowaw)

    def TT(eng, out, in0, in1, op, reads, writes, nowaw=True):
        S.op(eng, lambda e: e.tensor_tensor(out=out, in0=in0, in1=in1, op=op), reads, writes, nowaw=nowaw)

    def TS(eng, out, in0, s1, s2, op0, op1, reads, writes, nowaw=True):
        if s2 is None:
            S.op(eng, lambda e: e.tensor_scalar(out=out, in0=in0, scalar1=s1, scalar2=None, op0=op0), reads, writes, nowaw=nowaw)
        else:
            S.op(eng, lambda e: e.tensor_scalar(out=out, in0=in0, scalar1=s1, scalar2=s2, op0=op0, op1=op1), reads, writes, nowaw=nowaw)

    def STT(eng, out, in0, scalar, in1, op0, op1, reads, writes, nowaw=True):
        S.op(eng, lambda e: e.scalar_tensor_tensor(out=out, in0=in0, scalar=scalar, in1=in1, op0=op0, op1=op1), reads, writes, nowaw=nowaw)

    def CP(eng, out, in_, reads, writes, nowaw=True):
        if eng == "act":
            ACT(out, in_, AF.Copy, reads, writes, nowaw=nowaw)
        else:
            S.op(eng, lambda e: e.tensor_copy(out=out, in_=in_), reads, writes, nowaw=nowaw)

    def RCP(out, in_, reads, writes):
        S.op("dve", lambda e: e.reciprocal(out=out, in_=in_), reads, writes, nowaw=True)

    def MEMSET(eng, ap, val, writes):
        S.op(eng, lambda e: e.memset(ap, val), (), writes, nowaw=True)

    def DMA(q, out, in_, reads, writes, sem, slow=False):
        if slow:
            S.dma(q, lambda e: e.dma_start(out=out, in_=in_, allow_slow_non_contiguous=True), reads, writes, sem=sem, nowaw=True)
        else:
            S.dma(q, lambda e: e.dma_start(out=out, in_=in_), reads, writes, sem=sem, nowaw=True)

    evi = [0]

    def EVAC(out, in_, reads, writes):
        evi[0] += 1
        CP("act" if evi[0] % 2 else "dve", out, in_, reads, writes)

    class WS:
        def __init__(self):
            self.plan = []
            self.loaded = 0
            self.released = set()
            self.kinds = {}
            self.seen = set()
            self.pending = []
            self.scr = None
            self.scrb = None

        def add(self, tag, src, p, a, b):
            kind = tag.split("_", 1)[1] if tag.startswith("in") else tag[0:2] + tag[tag.index("_"):]
            if kind not in self.kinds:
                self.kinds[kind] = len(self.kinds)
            self.plan.append((tag, src, p, a, b, self.kinds[kind]))

        def finalize(self):
            nk = len(self.kinds)
            self.scr = dscr("WSCR", [nk, 128, 2048], BF16)
            self.scrb = [S.buf(f"wscr{i}") for i in range(nk)]

        def _store(self, jj):
            tag, src, p, a, b, ks = self.plan[jj]
            n = a * b
            DMA("sp", self.scr[ks][0:p, 0:n], wbf[jj % NB][0:p, 0:n], [wbb[jj % NB]], [self.scrb[ks]], dk("wst", 2))

        def _flush(self, upto):
            while self.pending and self.pending[0] <= upto:
                self._store(self.pending.pop(0))

        def _load(self, j):
            tag, src, p, a, b, ks = self.plan[j]
            n = a * b
            if tag not in self.seen:
                self.seen.add(tag)
                self._flush(j - 2)
                sv = stg[j % NS][0:p, 0:n].rearrange("p (a b) -> p a b", a=a)
                DMA("sp", sv, src, (), [stgb[j % NS]], sem=f"w{j % NS}")
                S.op("pool", lambda e: e.tensor_copy(out=wbf[j % NB][0:p, 0:n], in_=stg[j % NS][0:p, 0:n]),
                     [stgb[j % NS]], [wbb[j % NB]])
                self.pending.append(j)
            else:
                self._flush(j)
                DMA("sp", wbf[j % NB][0:p, 0:n], self.scr[ks][0:p, 0:n], [self.scrb[ks]], [wbb[j % NB]], sem=f"wb{j % NB}")

        def _pump(self, upto):
            while self.loaded < len(self.plan) and self.loaded <= upto:
                j = self.loaded
                if j >= NB and (j - NB) not in self.released:
                    break
                self._load(j)
                self.loaded += 1

        def get(self, i, tag):
            assert self.plan[i][0] == tag, (i, self.plan[i][0], tag)
            self._pump(i + NB - 1)
            assert self.loaded > i, f"weight chunk {i} {tag} not loadable (ring deadlock)"
            tg, src, p, a, b, ks = self.plan[i]
            return wbf[i % NB][0:p, 0:a * b].rearrange("p (a b) -> p a b", a=a), wbb[i % NB]

        def release(self, i):
            self.released.add(i)
            self._pump(i + NB)

    ws = WS()
    wi = [0]

    def wget(tag):
        i = wi[0]
        wi[0] += 1
        v, b = ws.get(i, tag)
        return i, v, b

    def plan_layer_A(l):
        v = w_in[l].rearrange("(kc p) c -> p kc c", p=128)
        for j in (3, 4, 5, 6, 7, 8):
            ws.add(f"in{l}_{j}", v[:, :, 256 * j:256 * j + 256], 128, 8, 256)

    def plan_layer_B(l):
        v = w_in[l].rearrange("(kc p) c -> p kc c", p=128)
        for j in (0, 1, 2):
            ws.add(f"in{l}_{j}", v[:, :, 256 * j:256 * j + 256], 128, 8, 256)
        for j in (13, 14, 15, 16):
            ws.add(f"in{l}_{j}", v[:, :, 256 * j:256 * j + 256], 128, 8, 256)
        ws.add(f"ws{l}_0", w_sT[l].rearrange("p (a b) -> p a b", a=8), 128, 8, 128)
        for j in (9, 10, 11, 12):
            ws.add(f"in{l}_{j}", v[:, :, 256 * j:256 * j + 256], 128, 8, 256)
        vb = w_pb[l].rearrange("(kc p) c -> p kc c", p=128)
        for mp in range(4):
            ws.add(f"in{l}_{17 + mp}", v[:, :, 256 * (17 + mp):256 * (18 + mp)], 128, 8, 256)
            ws.add(f"in{l}_{21 + mp}", v[:, :, 256 * (21 + mp):256 * (22 + mp)], 128, 8, 256)
            ws.add(f"pa{l}_{mp}", w_pa[l][:, :, 256 * mp:256 * mp + 256], 64, 4, 256)
            ws.add(f"pb{l}_{mp}", vb[:, :, 256 * mp:256 * mp + 256], 128, 8, 256)
        vo = w_o[l].rearrange("(kc p) c -> p kc c", p=128)
        for j in range(4):
            ws.add(f"wo{l}_{j}", vo[:, :, 256 * j:256 * j + 256], 128, 8, 256)

    def plan_layer_C(l):
        vu = w_up[l].rearrange("(kc p) c -> p kc c", p=128)
        for j in range(NPAIR):
            ws.add(f"up{l}_{j}", vu[:, :, 256 * j:256 * j + 256], 128, 8, 256)
        vd = w_dn[l].rearrange("(kc p) c -> p kc c", p=128)
        for m in range(8):
            for hf in range(2):
                ws.add(f"dn{l}_{m}_{hf}", vd[:, 11 * hf:11 * hf + 11, 128 * m:128 * m + 128], 128, 11, 128)

    for l in range(depth):
        for s in range(nseg):
            plan_layer_A(l)
        for s in range(nseg):
            plan_layer_B(l)
        for u in range(NU):
            plan_layer_C(l)

    ws.finalize()
    prev_bufs = []

    pass_cache = {}
    last_kind = [None]

    def new_pass(kind=None):
        W = Bump(WBASE, WLIM)
        if kind is not None and kind == last_kind[0]:
            cache = pass_cache[kind]

            def mkc(name):
                return cache[name]
            return W, mkc
        ev = S.fence(prev_bufs)
        prev_bufs.clear()
        cache = {}
        pass_cache[kind] = cache
        last_kind[0] = kind

        def mk(name):
            b = S.buf(name, init=ev)
            prev_bufs.append(b)
            cache[name] = b
            return b
        return W, mk

    def ck(name):
        if stop == name:
            S.emit()
            raise _Stop(nc)

    DMA("sp", ident_f, ident_in, (), [cb], "c0")
    DMA("sp", segmask, segmask_in, (), [cb], "c1")
    DMA("sp", convflag, convflag_in, (), [cb], "c2")
    DMA("sp", relb, relb_in.partition_broadcast(128), (), [cb], "c3")
    DMA("sp", nfg, nf_in, (), [cb], "c4")
    MEMSET("dve", ones_mean, 1.0 / 1024.0, [cb])
    MEMSET("dve", ones_bf, 1.0, [cb])
    MEMSET("dve", epscol, EPS, [cb])
    MEMSET("dve", zerocol, 0.0, [cb])

    W, mk = new_pass()
    Z = W([6144], BF16)
    Zb = mk("Z")
    MEMSET("pool", Z, 0.0, [Zb])
    for c in range(6):
        DMA("sp", KT[c * 128:(c + 1) * 128, 0:PAD], Z[:, 0:PAD], [Zb], [KTp], dk("z", 4))
        DMA("sp", KT[c * 128:(c + 1) * 128, PAD + T:PAD + T + PAD], Z[:, 0:PAD], [Zb], [KTp], dk("z", 4))
    Z3 = Z.rearrange("p (a c) -> p a c", a=8)
    DMA("sp", VS[0:PAD, :].rearrange("(a p) c -> p a c", p=128), Z3, [Zb], [VSp], dk("z", 4))
    DMA("sp", VS[PAD + T:PAD + T + PAD, :].rearrange("(a p) c -> p a c", p=128), Z3, [Zb], [VSp], dk("z", 4))
    Zf = Z[:, 0:16].bitcast(F32).rearrange("p (a o) -> p a o", o=1)
    for Xv, hb in ((XAv, XAh), (XBv, XBh)):
        DMA("sp", Xv[:, :, 0:1], Zf, [Zb], [hb], dk("z", 4), slow=True)
        DMA("sp", Xv[:, :, T + 1:T + 2], Zf, [Zb], [hb], dk("z", 4), slow=True)
    EP = [W([128], F32) for i in range(2)]
    EPb = [mk(f"ep{i}") for i in range(2)]
    ei = 0
    for g in range(3):
        for ty in range(2):
            lst = _PDESC[g * 2 + ty]
            for n_, (b, pi) in enumerate(lst):
                DMA("sp", EP[ei % 2], epl_in[pi], (), [EPb[ei % 2]], f"ep{ei % 2}")
                for h in range(4):
                    sc = NEG if b < 0 else relb[:, b * 12 + 4 * g + h:b * 12 + 4 * g + h + 1]
                    if n_ == 0:
                        TS("dve", TAB[g][:, 2 * h + ty, :], EP[ei % 2], sc, None, ALU.mult, None, [EPb[ei % 2], cb], [tabb])
                    else:
                        STT("dve", TAB[g][:, 2 * h + ty, :], EP[ei % 2], sc, TAB[g][:, 2 * h + ty, :], ALU.mult, ALU.add,
                            [EPb[ei % 2], cb, tabb], [tabb])
                ei += 1

    ck("init")
    def rmsnorm(xt, xb, n, gain, gb, out, ob, SQ, SQb, RS, RSb):
        ACT(SQ[:, :, 0:n], xt, AF.Square, [xb], [SQb])
        bk, bb = pb()
        for kc in range(8):
            MM(bk[:, 0:n], ones_mean, SQ[:, kc, 0:n], kc == 0, kc == 7, [SQb, cb], [bb])
        ACT(RS[:, 0:n], bk[:, 0:n], AF.Sqrt, [bb, cb], [RSb], bias=epscol)
        RCP(RS[:, 0:n], RS[:, 0:n], [RSb], [RSb])
        for kc in range(8):
            STT("dve", out[:, kc, :], xt[:, kc, :], gain[:, kc:kc + 1], RS[:, 0:n],
                ALU.mult, ALU.mult, [xb, RSb, gb], [ob])

    XIN = [W([1024], F32) for i in range(2)]
    XINb = [mk(f"xin{i}") for i in range(2)]
    XT0 = [W([8, 128], F32) for i in range(2)]
    XT0b = [mk(f"xt0{i}") for i in range(2)]
    for tt in range(T // 128):
        i2 = tt % 2
        DMA("sp", XIN[i2], x_in[tt * 128:(tt + 1) * 128, :], (), [XINb[i2]], f"xin{i2}")
        for half in range(2):
            bk, bb = pb()
            for q in range(4):
                kc = half * 4 + q
                TR(bk[:, q * 128:(q + 1) * 128], XIN[i2][:, kc * 128:(kc + 1) * 128], [XINb[i2]], [bb])
            EVAC(XT0[i2][:, half * 4:half * 4 + 4, :], bk.rearrange("p (a b) -> p a b", a=4), [bb], [XT0b[i2]])
        DMA(STQ, XAv[:, :, 1 + tt * 128:1 + (tt + 1) * 128], XT0[i2], [XT0b[i2]], [XAb[tt // 4]], dk("st", 4))

    ck("p0")
    for l in range(depth):
        DMA("sp", pp, pp_in[l], (), [ppb], "pp0")
        DMA("sp", vg_bc, vg_in[l].partition_broadcast(128), (), [ppb], "pp1")
        DMA("sp", bs_bc.rearrange("p a b -> p (a b)"), bs_in[l].partition_broadcast(128), (), [ppb], "pp2")

        for s in range(nseg):
            W, mk = new_pass("A")
            XS = [W([8, 512], F32) for i in range(2)]
            XSb = [mk(f"xs{i}") for i in range(2)]
            SQ = W([8, 512], BF16)
            SQb = mk("sq")
            RS = W([512], F32)
            RSb = mk("rs")
            HT2 = [W([8, SEG], BF16) for i in range(2)]
            HT2b = [mk(f"hts{i}") for i in range(2)]
            HTs, HTsb = HT2[s % 2], HT2b[s % 2]
            KST = [W([SEG], BF16) for i in range(2)]
            KSTb = [mk(f"kst{i}") for i in range(2)]
            VST = [W([16, 256], BF16) for i in range(2)]
            VSTb = [mk(f"vst{i}") for i in range(2)]
            for t in range(4):
                gt = s * 4 + t
                DMA("sp", XS[t % 2], XAv[:, :, 1 + gt * 512:1 + (gt + 1) * 512], [XAb[gt]], [XSb[t % 2]], f"xs{t % 2}")
                rmsnorm(XS[t % 2], XSb[t % 2], 512, pp[:, 0:8], ppb, HTs[:, :, t * 512:(t + 1) * 512], HTsb, SQ, SQb, RS, RSb)
            DMA(STQ, HTv[:, :, s * SEG:(s + 1) * SEG], HTs, [HTsb], [HTb[s]], dk("st", 4))
            ki = 0
            for j in (3, 4, 5):
                ci, wv, wb_ = wget(f"in{l}_{j}")
                g = j - 3
                for m in range(2):
                    for t in range(4):
                        bk, bb = pb()
                        for kc in range(8):
                            MM(bk, wv[:, kc, m * 128:(m + 1) * 128], HTs[:, kc, t * 512:(t + 1) * 512], kc == 0, kc == 7, [wb_, HTsb], [bb])
                        EVAC(KST[ki % 2][:, t * 512:(t + 1) * 512], bk, [bb], [KSTb[ki % 2]])
                    r0 = g * 256 + m * 128
                    DMA(STQ, KT[r0:r0 + 128, PAD + s * SEG:PAD + (s + 1) * SEG], KST[ki % 2], [KSTb[ki % 2]], [KTb[s]], dk("st", 4))
                    ki += 1
                ws.release(ci)
            for j in (6, 7, 8):
                ci, wv, wb_ = wget(f"in{l}_{j}")
                g = j - 6
                vi = j % 2
                for t2 in range(8):
                    bk, bb = pb()
                    for q in range(2):
                        tt = t2 * 2 + q
                        for kc in range(8):
                            MM(bk[:, q * 256:(q + 1) * 256], HTs[:, kc, tt * 128:(tt + 1) * 128], wv[:, kc, :], kc == 0, kc == 7, [wb_, HTsb], [bb])
                    EVAC(VST[vi][:, t2 * 2:t2 * 2 + 2, :], bk.rearrange("p (a b) -> p a b", a=2), [bb], [VSTb[vi]])
                DMA(STQ, VS[PAD + s * SEG:PAD + (s + 1) * SEG, g * 256:(g + 1) * 256].rearrange("(tt p) c -> p tt c", p=128),
                    VST[vi], [VSTb[vi]], [VSb[s]], dk("st", 4))
                ws.release(ci)

        ck("A")
        for s in range(nseg):
            W, mk = new_pass(None)
            HTs = W([8, SEG], BF16)
            HTsb = mk("hts")
            OT = W([4, SEG], BF16)
            OTb = mk("ot")
            RA = W.o
            QT = W([6, SEG], BF16)
            QTb = mk("qt")
            KW = [W([4096], BF16) for i in range(2)]
            KWb = [mk(f"kw{i}") for i in range(2)]
            VW = [W([32, 128], BF16) for i in range(2)]
            VWb = [mk(f"vw{i}") for i in range(2)]
            ACN = W([2, SEG], F32)
            ACNb = mk("acn")
            ACD = W([2, SEG], F32)
            ACDb = mk("acd")
            TMP = [W([2, 128], F32) for i in range(4)]
            TMPb = [mk(f"tmp{i}") for i in range(4)]
            PT = [W([4, 128], BF16) for i in range(2)]
            PTb = [mk(f"pt{i}") for i in range(2)]
            RAend = W.o

            DMA("sp", HTs, HTv[:, :, s * SEG:(s + 1) * SEG], [HTb[s]], [HTsb], "hts")
            for j in (0, 1, 2):
                ci, wv, wb_ = wget(f"in{l}_{j}")
                for m in range(2):
                    for t in range(4):
                        bk, bb = pb()
                        for kc in range(8):
                            MM(bk, wv[:, kc, m * 128:(m + 1) * 128], HTs[:, kc, t * 512:(t + 1) * 512], kc == 0, kc == 7, [wb_, HTsb], [bb])
                        EVAC(QT[:, 2 * j + m, t * 512:(t + 1) * 512], bk, [bb], [QTb])
                ws.release(ci)
            ck("B1q")
            kvi = 0
            tpi = 0
            for pair in range(2):
                for g in range(3):
                    d = DIL[g]
                    nqt = 16 // d
                    Wn = SEG + 128 * d
                    kv = kvi % 2
                    kvi += 1
                    r0 = g * 256 + pair * 128
                    c0 = PAD + s * SEG - 64 * d
                    kdeps = [KTb[s], KTp] + ([KTb[s - 1]] if s > 0 else []) + ([KTb[s + 1]] if s + 1 < nseg else [])
                    vdeps = [VSb[s], VSp] + ([VSb[s - 1]] if s > 0 else []) + ([VSb[s + 1]] if s + 1 < nseg else [])
                    DMA("sp", KW[kv][:, 0:Wn], KT[r0:r0 + 128, c0:c0 + Wn], kdeps, [KWb[kv]], f"kw{kv}")
                    for r in range(d):
                        src = VS[c0 + r:c0 + r + ((nqt + 1) * 128 - 1) * d + 1:d, r0:r0 + 128].rearrange("(j p) c -> p j c", p=128)
                        DMA("sp", VW[kv][:, r * (nqt + 1):(r + 1) * (nqt + 1), :], src, vdeps, [VWb[kv]], dk("vw", 4))
                    ck("B1l")
                    for r in range(d):
                        for qt in range(nqt):
                            q0 = r + qt * 128 * d
                            qsl = slice(q0, q0 + 127 * d + 1, d)
                            pi_ = tpi % 2
                            bks = [pb(), pb()]
                            for h2 in range(2):
                                bk, bb = bks[h2]
                                for kt in range(2):
                                    k0 = (qt + kt) * 128 * d + r
                                    ksl = slice(k0, k0 + 127 * d + 1, d)
                                    MM(bk[:, kt * 128:(kt + 1) * 128], KW[kv][h2 * 64:(h2 + 1) * 64, ksl],
                                       QT[h2 * 64:(h2 + 1) * 64, 2 * g + pair, qsl], True, True, [KWb[kv], QTb], [bb])
                            for h2 in range(2):
                                bk, bb = bks[h2]
                                ti = (tpi * 2 + h2) % 4
                                hh = pair * 2 + h2
                                STT("dve", TMP[ti], bk[:, 0:256].rearrange("p (a b) -> p a b", a=2), 0.125,
                                    TAB[g][:, 2 * hh:2 * hh + 2, :], ALU.mult, ALU.add, [bb, tabb], [TMPb[ti]])
                                if qt != 0 and qt != nqt - 1:
                                    ACT(PT[pi_][:, 2 * h2:2 * h2 + 2, :], TMP[ti], AF.Exp, [TMPb[ti], cb], [PTb[pi_]], bias=zerocol)
                                else:
                                    for kt in range(2):
                                        if qt == 0 and kt == 0:
                                            bcol = segmask[:, 2 * s:2 * s + 1]
                                        elif qt == nqt - 1 and kt == 1:
                                            bcol = segmask[:, 2 * s + 1:2 * s + 2]
                                        else:
                                            bcol = zerocol
                                        ACT(PT[pi_][:, 2 * h2 + kt, :], TMP[ti][:, kt, :], AF.Exp, [TMPb[ti], cb], [PTb[pi_]], bias=bcol)
                            tpi += 1
                            bkn, bbn = pb()
                            for h2 in range(2):
                                for kt in range(2):
                                    vt = r * (nqt + 1) + qt + kt
                                    MM(bkn[0:64, h2 * 128:(h2 + 1) * 128], VW[kv][:, vt, h2 * 64:(h2 + 1) * 64], PT[pi_][:, 2 * h2 + kt, :],
                                       kt == 0, kt == 1, [VWb[kv], PTb[pi_]], [bbn])
                            for h2 in range(2):
                                for kt in range(2):
                                    MM(bkn[0:64, 256 + h2 * 128:256 + (h2 + 1) * 128], ones_bf[:, 0:64], PT[pi_][:, 2 * h2 + kt, :],
                                       kt == 0, kt == 1, [cb, PTb[pi_]], [bbn])
                            ck("B1m")
                            nv = bkn[0:64, 0:256].rearrange("p (a b) -> p a b", a=2)
                            dv = bkn[0:64, 256:512].rearrange("p (a b) -> p a b", a=2)
                            an = ACN[0:64, :, qsl]
                            ad = ACD[0:64, :, qsl]
                            if g == 0:
                                CP("act", an, nv, [bbn], [ACNb])
                                CP("act", ad, dv, [bbn], [ACDb])
                            else:
                                TT("dve", an, an, nv, ALU.add, [bbn, ACNb], [ACNb])
                                TT("pool" if False else "dve", ad, ad, dv, ALU.add, [bbn, ACDb], [ACDb])
                    ck("B1g")
                for t in range(4):
                    sl = slice(t * 512, (t + 1) * 512)
                    RCP(ACD[0:64, :, sl], ACD[0:64, :, sl], [ACDb], [ACDb])
                    TT("dve", OT[0:64, pair * 2:pair * 2 + 2, sl], ACN[0:64, :, sl], ACD[0:64, :, sl], ALU.mult, [ACNb, ACDb], [OTb])

            ck("B1")
            ev2 = S.fence([QTb, ACNb, ACDb] + KWb + VWb + TMPb + PTb)
            W2 = Bump(RA, WLIM)

            def mk2(name):
                b = S.buf(name, init=ev2)
                prev_bufs.append(b)
                return b
            VN = W2([16, 1024], BF16)
            VNb = mk2("vn")
            SG = W2([8, SEG], BF16)
            SGb = mk2("sg")
            GV = [W2([1024], F32) for i in range(2)]
            GVb = [mk2(f"gv{i}") for i in range(2)]
            SQV = W2([1024], BF16)
            SQVb = mk2("sqv")
            SS = [W2([1], F32) for i in range(2)]
            SSb = [mk2(f"ss{i}") for i in range(2)]
            UT = [W2([512], BF16) for i in range(2)]
            UTb = [mk2(f"ut{i}") for i in range(2)]
            ZT = [W2([512], F32) for i in range(2)]
            ZTb = [mk2(f"zt{i}") for i in range(2)]
            SGA = [W2([512], F32) for i in range(2)]
            SGAb = [mk2(f"sga{i}") for i in range(2)]
            SGB = [W2([512], F32) for i in range(2)]
            SGBb = [mk2(f"sgb{i}") for i in range(2)]
            T1 = [W2([512], F32) for i in range(2)]
            T1b = [mk2(f"t1{i}") for i in range(2)]
            T2 = [W2([512], F32) for i in range(2)]
            T2b = [mk2(f"t2{i}") for i in range(2)]
            vch = [wget(f"in{l}_{j}") for j in (13, 14, 15, 16)]
            for tt in range(16):
                i2 = tt % 2
                bks = [pb(), pb()]
                for c4 in range(4):
                    ci, wv, wb_ = vch[c4]
                    bk, bb = bks[c4 // 2]
                    for kc in range(8):
                        MM(bk[:, (c4 % 2) * 256:(c4 % 2 + 1) * 256], HTs[:, kc, tt * 128:(tt + 1) * 128], wv[:, kc, :], kc == 0, kc == 7,
                           [wb_, HTsb], [bb])
                for hb in range(2):
                    ACT(GV[i2][:, hb * 512:(hb + 1) * 512], bks[hb][0], AF.Gelu_apprx_tanh, [bks[hb][1]], [GVb[i2]])
                ACT(SQV, GV[i2], AF.Square, [GVb[i2]], [SQVb])
                S.op("dve", lambda e, o=SS[i2], i=SQV: e.reduce_sum(out=o, in_=i, axis=AX.X), [SQVb], [SSb[i2]], nowaw=True)
                ACT(SS[i2], SS[i2], AF.Sqrt, [SSb[i2], cb], [SSb[i2]], bias=epscol, scale=1.0 / 1024.0)
                RCP(SS[i2], SS[i2], [SSb[i2]], [SSb[i2]])
                STT("dve", VN[:, tt, :], GV[i2], SS[i2], vg_bc, ALU.mult, ALU.mult, [GVb[i2], SSb[i2], ppb], [VNb])
            for ci, wv, wb_ in vch:
                ws.release(ci)
            cws, wsv, wsb = wget(f"ws{l}_0")
            ui = 0
            for gp in range(4):
                ci, wv, wb_ = wget(f"in{l}_{9 + gp}")
                for gg in range(2):
                    g8 = gp * 2 + gg
                    for t in range(4):
                        sl = slice(t * 512, (t + 1) * 512)
                        i2 = ui % 2
                        ui += 1
                        bk, bb = pb()
                        for kc in range(8):
                            MM(bk, wv[:, kc, gg * 128:(gg + 1) * 128], HTs[:, kc, sl], kc == 0, kc == 7, [wb_, HTsb], [bb])
                        ACT(UT[i2], bk, AF.Gelu_apprx_tanh, [bb], [UTb[i2]])
                        bz, bzb = pb()
                        for n4 in range(4):
                            MM(bz[:, n4 * 128:(n4 + 1) * 128], VN[:, t * 4 + n4, g8 * 128:(g8 + 1) * 128], wsv[:, g8, :], True, True,
                               [VNb, wsb], [bzb])
                        TT("dve", ZT[i2].rearrange("p (a b) -> p a b", a=4), bz.rearrange("p (a b) -> p a b", a=4),
                           bs_bc[:, g8, :].unsqueeze(1).broadcast_to([128, 4, 128]), ALU.add, [bzb, ppb], [ZTb[i2]])
                        TT("pool", SG[:, g8, sl], ZT[i2], UT[i2], ALU.mult, [ZTb[i2], UTb[i2]], [SGb])
                ws.release(ci)
            ws.release(cws)
            MG = VN.rearrange("p a b -> p (a b)").rearrange("p (a b) -> p a b", a=8)
            MGb = S.buf("mg", init=S.fence([VNb]))
            prev_bufs.append(MGb)
            mi = 0
            for mp in range(4):
                cga, wga, bga = wget(f"in{l}_{17 + mp}")
                cgb, wgb, bgb = wget(f"in{l}_{21 + mp}")
                cpa, wpa, bpa = wget(f"pa{l}_{mp}")
                cpb, wpb, bpb = wget(f"pb{l}_{mp}")
                for mm in range(2):
                    m = mp * 2 + mm
                    ms = slice(mm * 128, (mm + 1) * 128)
                    for t in range(4):
                        sl = slice(t * 512, (t + 1) * 512)
                        i2 = mi % 2
                        mi += 1
                        bk, bb = pb()
                        for kc in range(8):
                            MM(bk, wga[:, kc, ms], HTs[:, kc, sl], kc == 0, kc == 7, [bga, HTsb], [bb])
                        ACT(SGA[i2], bk, AF.Sigmoid, [bb], [SGAb[i2]])
                        bk, bb = pb()
                        for kc in range(8):
                            MM(bk, wgb[:, kc, ms], HTs[:, kc, sl], kc == 0, kc == 7, [bgb, HTsb], [bb])
                        ACT(SGB[i2], bk, AF.Sigmoid, [bb], [SGBb[i2]])
                        bk, bb = pb()
                        for h in range(4):
                            MM(bk, wpa[0:64, h, ms], OT[0:64, h, sl], h == 0, h == 3, [bpa, OTb], [bb])
                        TT("dve", T1[i2], bk, SGA[i2], ALU.mult, [bb, SGAb[i2]], [T1b[i2]])
                        bk, bb = pb()
                        for kc in range(8):
                            MM(bk, wpb[:, kc, ms], SG[:, kc, sl], kc == 0, kc == 7, [bpb, SGb], [bb])
                        TT("dve", T2[i2], bk, SGB[i2], ALU.mult, [bb, SGBb[i2]], [T2b[i2]])
                        TT("pool", MG[:, m, sl], T1[i2], T2[i2], ALU.add, [T1b[i2], T2b[i2]], [MGb])
                for c_ in (cga, cgb, cpa, cpb):
                    ws.release(c_)
            evx = S.fence([HTsb])
            XS = [HTs.rearrange("p a b -> p (a b)")[:, i * 8192:(i + 1) * 8192].bitcast(F32).rearrange("p (a b) -> p a b", a=8) for i in range(2)]
            XSb = []
            for i in range(2):
                b_ = S.buf(f"xsB{i}", init=evx)
                prev_bufs.append(b_)
                XSb.append(b_)
            och = [wget(f"wo{l}_{j}") for j in range(4)]
            for t in range(4):
                gt = s * 4 + t
                i2 = t % 2
                sl = slice(t * 512, (t + 1) * 512)
                DMA("sp", XS[i2], XAv[:, :, 1 + gt * 512:1 + (gt + 1) * 512], [XAb[gt]], [XSb[i2]], f"xsB{i2}")
                for m in range(8):
                    ci, wv, wb_ = och[m // 2]
                    bk, bb = pb()
                    for kc in range(8):
                        MM(bk, wv[:, kc, (m % 2) * 128:(m % 2 + 1) * 128], MG[:, kc, sl], kc == 0, kc == 7, [wb_, MGb], [bb])
                    TT("dve", XS[i2][:, m, :], bk, XS[i2][:, m, :], ALU.add, [bb, XSb[i2]], [XSb[i2]])
                DMA(STQ, XBv[:, :, 1 + gt * 512:1 + (gt + 1) * 512], XS[i2], [XSb[i2]], [XBb[gt]], dk("st", 4))
            for ci, wv, wb_ in och:
                ws.release(ci)

        ck("B")
        last = (l == depth - 1)
        for u in range(NU):
            W, mk = new_pass("C")
            XU = [W([8, 512], F32) for i in range(2)]
            XUb = [mk(f"xu{i}") for i in range(2)]
            XH = W([8, 2], F32)
            XHb = mk("xh")
            SQ = W([8, 512], BF16)
            SQb = mk("sq")
            RS = W([512], F32)
            RSb = mk("rs")
            H2 = W([8, 1026], BF16)
            H2b = mk("h2")
            GOFF = W.o
            G = W([NPAIR, UNIT], BF16)
            Gb = mk("g")
            RC = W.o
            AG = [W([1026], F32) for i in range(2)]
            AGb = [mk(f"ag{i}") for i in range(2)]
            AV = [W([1026], F32) for i in range(2)]
            AVb = [mk(f"av{i}") for i in range(2)]
            CG = [W([UNIT], F32) for i in range(2)]
            CGb = [mk(f"cg{i}") for i in range(2)]
            CV = [W([UNIT], F32) for i in range(2)]
            CVb = [mk(f"cv{i}") for i in range(2)]
            c0 = u * UNIT
            for t in range(2):
                gt = u * 2 + t
                DMA("sp", XU[t], XBv[:, :, 1 + gt * 512:1 + (gt + 1) * 512], [XBb[gt]], [XUb[t]], f"xu{t}")
            hdeps = [XBh] + ([XBb[u * 2 - 1]] if u > 0 else []) + ([XBb[u * 2 + 2]] if u * 2 + 2 < T // 512 else [])
            DMA("sp", XH[:, :, 0:1], XBv[:, :, c0:c0 + 1], hdeps, [XHb], "xh0", slow=True)
            DMA("sp", XH[:, :, 1:2], XBv[:, :, c0 + UNIT + 1:c0 + UNIT + 2], hdeps, [XHb], "xh1", slow=True)
            for t in range(2):
                rmsnorm(XU[t], XUb[t], 512, pp[:, 8:16], ppb, H2[:, :, 1 + t * 512:1 + (t + 1) * 512], H2b, SQ, SQb, RS, RSb)
            HH = W([8, 2], F32)
            HHb = mk("hh")
            rmsnorm(XH, XHb, 2, pp[:, 8:16], ppb, HH, HHb, SQ, SQb, RS, RSb)
            for side in range(2):
                col = 0 if side == 0 else 1025
                TS("dve", H2[:, :, col:col + 1], HH[:, :, side:side + 1], convflag[:, 2 * u + side:2 * u + side + 1], None, ALU.mult, None,
                   [HHb, cb], [H2b])
            pieces = ((0, 512), (512, 512), (1024, 2))
            for j in range(NPAIR):
                ci, wv, wb_ = wget(f"up{l}_{j}")
                i2 = j % 2
                for m in range(2):
                    A_, Ab_ = (AG[i2], AGb[i2]) if m == 0 else (AV[i2], AVb[i2])
                    for (o0, n) in pieces:
                        bk, bb = pb()
                        for kc in range(8):
                            MM(bk[:, 0:n], wv[:, kc, m * 128:(m + 1) * 128], H2[:, kc, o0:o0 + n], kc == 0, kc == 7, [wb_, H2b], [bb])
                        CP("act", A_[:, o0:o0 + n], bk[:, 0:n], [bb], [Ab_])
                        C_, Cb_ = (CG[i2], CGb[i2]) if m == 0 else (CV[i2], CVb[i2])
                        cc = 2 * j + m
                        lo = 1 if o0 == 0 else 0
                        hi = min(o0 + n, 1025) - o0
                        S.op("act", lambda e, o=C_[:, o0 + lo - 1:o0 + hi - 1], i=bk[:, lo:hi], sc=pp[:, 60 + cc:61 + cc], bi=pp[:, 148 + cc:149 + cc]:
                             e.activation(out=o, in_=i, func=AF.Identity, bias=bi, scale=sc), [bb, ppb], [Cb_], nowaw=True)
                ws.release(ci)
                for m in range(2):
                    A_, Ab_ = (AG[i2], AGb[i2]) if m == 0 else (AV[i2], AVb[i2])
                    C_, Cb_ = (CG[i2], CGb[i2]) if m == 0 else (CV[i2], CVb[i2])
                    eng = "dve"
                    cc = 2 * j + m
                    w0 = pp[:, 16 + cc:17 + cc]
                    w1 = pp[:, 60 + cc:61 + cc]
                    w2 = pp[:, 104 + cc:105 + cc]
                    bcv = pp[:, 148 + cc:149 + cc]
                    STT(eng, C_, A_[:, 0:1024], w0, C_, ALU.mult, ALU.add, [Ab_, ppb, Cb_], [Cb_])
                    STT(eng, C_, A_[:, 2:1026], w2, C_, ALU.mult, ALU.add, [Ab_, ppb, Cb_], [Cb_])
                ACT(CG[i2], CG[i2], AF.Gelu_apprx_tanh, [CGb[i2]], [CGb[i2]])
                TT("dve", G[:, j, :], CG[i2], CV[i2], ALU.mult, [CGb[i2], CVb[i2]], [Gb])
            rcb = AGb + AVb + CGb + CVb
            evr = S.fence(rcb)
            WR = Bump(RC, WLIM)
            XR = [WR([8, 512], F32) for i in range(2)]
            XRb = [S.buf(f"xr{i}", init=evr) for i in range(2)]
            for t in range(2):
                gt = u * 2 + t
                DMA("sp", XR[t], XBv[:, :, 1 + gt * 512:1 + (gt + 1) * 512], [XBb[gt]], [XRb[t]], f"xr{t}")
            for m in range(8):
                bks = [pb(), pb()]
                for hf in range(2):
                    ci, wv, wb_ = wget(f"dn{l}_{m}_{hf}")
                    for t in range(2):
                        bk, bb = bks[t]
                        for k11 in range(11):
                            kc = hf * 11 + k11
                            MM(bk, wv[:, k11, :], G[:, kc, t * 512:(t + 1) * 512], kc == 0, kc == 21, [wb_, Gb], [bb])
                    ws.release(ci)
                for t in range(2):
                    TT("dve", XR[t][:, m, :], bks[t][0], XR[t][:, m, :], ALU.add, [bks[t][1], XRb[t]], [XRb[t]])
            if not last:
                for t in range(2):
                    gt = u * 2 + t
                    DMA(STQ, XAv[:, :, 1 + gt * 512:1 + (gt + 1) * 512], XR[t], [XRb[t]], [XAb[gt]], dk("st", 4))
                S.absorb(rcb, XRb)
            else:
                evy = S.fence([Gb])
                WY = Bump(GOFF, WLIM)
                YT = WY([8, 512], F32)
                YTb = S.buf("yt", init=evy)
                YO = [WY([1024], F32) for i in range(2)]
                YOb = [S.buf(f"yo{i}", init=evy) for i in range(2)]
                for t in range(2):
                    gt = u * 2 + t
                    rmsnorm(XR[t], XRb[t], 512, nfg, cb, YT, YTb, SQ, SQb, RS, RSb)
                    for q in range(4):
                        tt = gt * 4 + q
                        i2 = q % 2
                        for half in range(2):
                            bk, bb = pb()
                            for c4 in range(4):
                                kc = half * 4 + c4
                                TR(bk[:, c4 * 128:(c4 + 1) * 128], YT[:, kc, q * 128:(q + 1) * 128], [YTb], [bb])
                            EVAC(YO[i2][:, half * 512:(half + 1) * 512], bk, [bb], [YOb[i2]])
                        DMA(STQ, y_out[tt * 128:(tt + 1) * 128, :], YO[i2], [YOb[i2]], (), dk("yst", 4))
                S.absorb(rcb, XRb)
                S.absorb([Gb], [YTb] + YOb)

    assert wi[0] == len(ws.plan), (wi[0], len(ws.plan))
    S.emit()
    return nc


NSEG_FULL = 6
DEPTH_FULL = 4


def _core_layout():
    cores = []
    for c in range(8):
        segs = []
        if c < 4:
            for k in range(4):
                segs.append(("p", c, k * SEG))
            segs.append(("s", 2 * c, 0))
            segs.append(("s", 2 * c + 1, 0))
        else:
            for k in range(6):
                segs.append(("s", 8 + 6 * (c - 4) + k, 0))
        cores.append(segs)
    return cores


def _flags(segs):
    n = len(segs)
    soft_start = []
    for i, sg in enumerate(segs):
        if i > 0 and sg[0] == "p" and segs[i - 1][0] == "p" and segs[i - 1][1] == sg[1] and segs[i - 1][2] + SEG == sg[2]:
            soft_start.append(True)
        else:
            soft_start.append(False)
    soft_end = [(i + 1 < n and soft_start[i + 1]) for i in range(n)]
    segmask = np.zeros((128, 2 * n), np.float32)
    nu = n * (SEG // UNIT)
    convflag = np.zeros((128, 2 * nu), np.float32)
    for i in range(n):
        if not soft_start[i]:
            segmask[0:64, 2 * i] = NEG
        if not soft_end[i]:
            segmask[64:128, 2 * i + 1] = NEG
        for uu in range(SEG // UNIT):
            u = i * (SEG // UNIT) + uu
            st = soft_start[i] if uu == 0 else True
            en = soft_end[i] if uu == SEG // UNIT - 1 else True
            convflag[:, 2 * u] = 1.0 if st else 0.0
            convflag[:, 2 * u + 1] = 1.0 if en else 0.0
    return segmask, convflag


def _prep_shared(depth, rel_bias, norm_mix, w_in, v_gain, w_s, b_s, w_proj_a, w_proj_b, w_out,
                 norm_ffn, w_up, conv_w, conv_b, w_down, norm_final):
    f = lambda a: np.ascontiguousarray(np.asarray(a, dtype=np.float32))
    L = depth
    sh = {}
    sh["relb"] = f(rel_bias).reshape(1, 384)
    sh["epl"] = _PLANES
    sh["ident"] = np.eye(128, dtype=np.float32)
    sh["w_in"] = f(w_in[:L])
    sh["w_sT"] = f(np.transpose(np.asarray(w_s[:L]), (0, 3, 1, 2)).reshape(L, 128, 1024))
    sh["w_pa"] = f(np.asarray(w_proj_a[:L]).reshape(L, 4, 64, 1024).transpose(0, 2, 1, 3))
    sh["w_pb"] = f(w_proj_b[:L])
    sh["w_o"] = f(w_out[:L])
    wu = np.asarray(w_up[:L]).reshape(L, D, 2, NPAIR, 128).transpose(0, 1, 3, 2, 4).reshape(L, D, 2 * DFF)
    sh["w_up"] = f(wu)
    sh["w_dn"] = f(w_down[:L])
    pp = np.zeros((L, 128, 192), np.float32)
    pp[:, :, 0:8] = np.asarray(norm_mix[:L]).reshape(L, 8, 128).transpose(0, 2, 1)
    pp[:, :, 8:16] = np.asarray(norm_ffn[:L]).reshape(L, 8, 128).transpose(0, 2, 1)
    cw = np.asarray(conv_w[:L]).reshape(L, 3, 2, NPAIR, 128).transpose(0, 1, 4, 3, 2).reshape(L, 3, 128, 44)
    for tap in range(3):
        pp[:, :, 16 + 44 * tap:16 + 44 * (tap + 1)] = cw[:, tap]
    cbv = np.asarray(conv_b[:L]).reshape(L, 2, NPAIR, 128).transpose(0, 3, 2, 1).reshape(L, 128, 44)
    pp[:, :, 148:192] = cbv
    sh["pp"] = pp
    sh["nf"] = f(np.asarray(norm_final).reshape(8, 128).T)
    sh["vg"] = f(np.asarray(v_gain[:L]).reshape(L, 1, 1024))
    sh["bs"] = f(np.asarray(b_s[:L]).reshape(L, 1, 1024))
    return sh


_NC_CACHE = {}


def kernel(x_prompt, x_sample, rel_bias, norm_mix, w_in, v_gain, w_s, b_s, w_proj_a, w_proj_b,
           w_out, norm_ffn, w_up, conv_w, conv_b, w_down, norm_final):
    xp = np.asarray(x_prompt, dtype=np.float32)
    xs = np.asarray(x_sample, dtype=np.float32)
    sh = _prep_shared(DEPTH_FULL, rel_bias, norm_mix, w_in, v_gain, w_s, b_s, w_proj_a, w_proj_b, w_out,
                      norm_ffn, w_up, conv_w, conv_b, w_down, norm_final)
    cores = _core_layout()
    in_maps = []
    for segs in cores:
        xc = np.concatenate([(xp if w == "p" else xs)[b, st:st + SEG] for (w, b, st) in segs], axis=0)
        sm, cf = _flags(segs)
        m = dict(sh)
        m["x"] = np.ascontiguousarray(xc)
        m["segmask"] = sm
        m["convflag"] = cf
        in_maps.append(m)
    key = (NSEG_FULL, DEPTH_FULL)
    if key not in _NC_CACHE:
        _NC_CACHE[key] = build(*key)
    nc = _NC_CACHE[key]
    res = run_bass_kernel_spmd(nc, in_maps, core_ids=list(range(8)))
    yp = np.zeros_like(xp)
    ys = np.zeros_like(xs)
    for c, segs in enumerate(cores):
        yc = res.results[c]["y"]
        for i, (w, b, st) in enumerate(segs):
            (yp if w == "p" else ys)[b, st:st + SEG] = yc[i * SEG:(i + 1) * SEG]
    return (yp, ys)
```

```python
import contextlib
import os
_SKIP = os.environ.get('KSKIP', '')
import math
import numpy as np
import concourse.bass as bass
import concourse.mybir as mybir
from concourse.bass_utils import run_bass_kernel_spmd

F32 = mybir.dt.float32
BF16 = mybir.dt.bfloat16
AF = mybir.ActivationFunctionType
ALU = mybir.AluOpType
AX = mybir.AxisListType

D = 1024
SEG = 2048
PAD = 1024
UNIT = 1024
DIL = (1, 4, 16)
NEG = -30000.0
EPS = 1e-6
DFF = 2816
NPAIR = 22
STQ = "pool"


class Buf:
    __slots__ = ("name", "w", "r", "pr")

    def __init__(self, name="", init=None):
        self.name = name
        self.w = {}
        self.r = dict(init) if init else {}
        self.pr = {}


def _merge(d, k, v):
    if d.get(k, 0) < v:
        d[k] = v


class Sched:
    CH = 8000
    ENGS = ("pe", "act", "dve", "pool", "sp")

    def __init__(self, nc, same_engine_sync=True):
        self.nc = nc
        self.ops = {e: [] for e in self.ENGS}
        self.count = {e: 0 for e in self.ENGS}
        self.waited = {}
        self.maxep = {}
        self.dcum = {}
        self.same = same_engine_sync
        self.nwait = 0
        self.allbufs = []

    def buf(self, name="", init=None):
        b = Buf(name, init)
        self.allbufs.append(b)
        return b

    def fence(self, bufs):
        ev = {}
        for b in bufs:
            for k, v in b.w.items():
                _merge(ev, k, v)
            for k, v in b.r.items():
                _merge(ev, k, v)
            for k, v in b.pr.items():
                _merge(ev, k, v)
        return ev

    def absorb(self, targets, sources):
        ev = self.fence(sources)
        for t in targets:
            for k, v in ev.items():
                _merge(t.r, k, v)

    def _deps(self, eng, reads, writes, nowaw):
        deps = {}
        for b in reads:
            for k, v in b.w.items():
                _merge(deps, k, v)
        for b in writes:
            for k, v in b.r.items():
                _merge(deps, k, v)
            for k, v in b.pr.items():
                _merge(deps, k, v)
            if not nowaw:
                for k, v in b.w.items():
                    _merge(deps, k, v)
        waits = []
        for k, v in deps.items():
            if k[0] == "e":
                if k[1] == eng and (eng == "pe" or not self.same):
                    continue
                if self.maxep.get((eng, k[1]), -1) > k[2]:
                    continue
            if self.waited.get((eng, k), 0) >= v:
                continue
            self.waited[(eng, k)] = v
            if k[0] == "e":
                self.maxep[(eng, k[1])] = max(self.maxep.get((eng, k[1]), -1), k[2])
            waits.append((k, v))
        return waits

    def _update(self, ev, reads, writes, nowaw):
        k, v = ev
        for b in reads:
            _merge(b.r, k, v)
        for b in writes:
            if b.r or not nowaw:
                b.w = {k: v}
                if b.r:
                    b.pr = b.r
                b.r = {}
            else:
                _merge(b.w, k, v)

    def op(self, eng, fn, reads=(), writes=(), nowaw=False):
        waits = self._deps(eng, reads, writes, nowaw)
        n = self.count[eng]
        self.count[eng] = n + 1
        ev = (("e", eng, n // self.CH), n % self.CH + 1)
        self.ops[eng].append((waits, fn, ev[0], 1))
        self._update(ev, reads, writes, nowaw)
        self.nwait += len(waits)

    def dma(self, q, fn, reads=(), writes=(), sem=None, nowaw=False):
        key = ("d", sem)
        waits = self._deps(q, reads, writes, nowaw)
        prev = self.dcum.get(key, 0)
        if prev and self.waited.get((q, key), 0) < prev:
            self.waited[(q, key)] = prev
            waits.append((key, prev))
        cur = prev + 16
        assert cur < 60000, f"dma sem overflow {sem}"
        self.dcum[key] = cur
        self.ops[q].append((waits, fn, key, 16))
        self._update((key, cur), reads, writes, nowaw)
        self.nwait += len(waits)

    def emit(self):
        nc = self.nc
        keys = set()
        for e in self.ENGS:
            for waits, fn, k, inc in self.ops[e]:
                keys.add(k)
        keys = sorted(keys, key=str)
        print(f"[sched] ops={ {e: len(self.ops[e]) for e in self.ENGS} } waits={self.nwait} sems={len(keys)}", flush=True)
        with contextlib.ExitStack() as st:
            sems = {}
            for i, k in enumerate(keys):
                sems[k] = st.enter_context(nc.semaphore(f"s{i}"))
            block = st.enter_context(nc.Block())
            finals = list(self.dcum.items())

            def replay(eng, e):
                for waits, fn, k, inc in self.ops[eng]:
                    for wk, wv in waits:
                        e.wait_ge(sems[wk], wv)
                    fn(e).then_inc(sems[k], inc)
                if eng == "sp":
                    for k, v in finals:
                        e.wait_ge(sems[k], v)

            @block.tensor
            def _(e):
                replay("pe", e)

            @block.scalar
            def _(e):
                replay("act", e)

            @block.vector
            def _(e):
                replay("dve", e)

            @block.gpsimd
            def _(e):
                replay("pool", e)

            @block.sync
            def _(e):
                replay("sp", e)


def _t5_bucket(rel):
    nb = 16
    max_exact = 8
    ret = np.where(rel > 0, nb, 0)
    n = np.abs(rel)
    nf = np.maximum(n, 1).astype(np.float32)
    large = max_exact + (np.log(nf / max_exact) / math.log(1024 / max_exact) * (nb - max_exact)).astype(np.int32)
    large = np.minimum(large, nb - 1)
    return (ret + np.where(n < max_exact, n, large)).astype(np.int32)


def _planes():
    planes = []
    desc = []
    p = np.arange(128)[:, None]
    n = np.arange(128)[None, :]
    for g, d in enumerate(DIL):
        for ty, c0 in enumerate((-64, 64)):
            rel = p - n + c0
            band = np.abs(rel) <= 64
            bk = _t5_bucket(rel * d)
            lst = []
            for b in range(32):
                m = band & (bk == b)
                if m.any():
                    lst.append((b, len(planes)))
                    planes.append(m.astype(np.float32))
            lst.append((-1, len(planes)))
            planes.append((~band).astype(np.float32))
            desc.append(lst)
    return np.stack(planes), desc


_PLANES, _PDESC = _planes()


class _Stop(Exception):
    pass


def build(nseg, depth, debug=False, stop=None):
    try:
        return _build(nseg, depth, debug, stop)
    except _Stop as e:
        return e.args[0]


def _build(nseg, depth, debug=False, stop=None):
    T = nseg * SEG
    NU = T // UNIT
    nc = bass.Bass("TRN2", target_bir_lowering=False)
    S = Sched(nc)

    def din(name, shape, dt=F32):
        return nc.dram_tensor(name, list(shape), dt, kind="ExternalInput").ap()

    x_in = din("x", [T, D])
    segmask_in = din("segmask", [128, 2 * nseg])
    convflag_in = din("convflag", [128, 2 * NU])
    relb_in = din("relb", [1, 384])
    epl_in = din("epl", list(_PLANES.shape))
    ident_in = din("ident", [128, 128])
    w_in = din("w_in", [depth, D, 6400])
    w_sT = din("w_sT", [depth, 128, 1024])
    w_pa = din("w_pa", [depth, 64, 4, 1024])
    w_pb = din("w_pb", [depth, D, D])
    w_o = din("w_o", [depth, D, D])
    w_up = din("w_up", [depth, D, 2 * DFF])
    w_dn = din("w_dn", [depth, DFF, D])
    pp_in = din("pp", [depth, 128, 192])
    nf_in = din("nf", [128, 8])
    vg_in = din("vg", [depth, 1, 1024])
    bs_in = din("bs", [depth, 1, 1024])
    y_out = nc.dram_tensor("y", [T, D], F32, kind="ExternalOutput").ap()

    def dscr(name, shape, dt):
        return nc.dram_tensor(name, list(shape), dt, kind="ExternalOutput" if debug else "Internal").ap()

    XA = dscr("XA", [D, T + 2], F32)
    XB = dscr("XB", [D, T + 2], F32)
    HT = dscr("HT", [D, T], BF16)
    KT = dscr("KT", [768, PAD + T + PAD], BF16)
    VS = dscr("VS", [PAD + T + PAD, 768], BF16)
    XAv = XA.rearrange("(kc p) t -> p kc t", p=128)
    XBv = XB.rearrange("(kc p) t -> p kc t", p=128)
    HTv = HT.rearrange("(kc p) t -> p kc t", p=128)
    XAb = [S.buf(f"XA{i}") for i in range(T // 512)]
    XBb = [S.buf(f"XB{i}") for i in range(T // 512)]
    XAh, XBh = S.buf("XAh"), S.buf("XBh")
    HTb = [S.buf(f"HT{i}") for i in range(nseg)]
    KTb = [S.buf(f"KT{i}") for i in range(nseg)]
    VSb = [S.buf(f"VS{i}") for i in range(nseg)]
    KTp, VSp = S.buf("KTp"), S.buf("VSp")

    TOTAL = 207 * 1024
    M = nc.alloc_sbuf_tensor("M", [128, TOTAL // 2], BF16).ap()

    def view(off, shape, dt, parts=128):
        n = int(np.prod(shape))
        nb = n * (4 if dt == F32 else 2)
        assert off % 4 == 0
        v = M[0:parts, off // 2:(off + nb) // 2]
        if dt == F32:
            v = v.bitcast(F32)
        if len(shape) == 2:
            v = v.rearrange("p (a b) -> p a b", a=shape[0])
        elif len(shape) == 3:
            v = v.rearrange("p (a b c) -> p a b c", a=shape[0], b=shape[1])
        return v

    class Bump:
        def __init__(self, base, limit):
            self.o = base
            self.limit = limit

        def __call__(self, shape, dt, parts=128):
            n = int(np.prod(shape)) * (4 if dt == F32 else 2)
            n = (n + 31) // 32 * 32
            off = self.o
            self.o += n
            assert self.o <= self.limit, f"sbuf overflow {self.o} > {self.limit}"
            return view(off, shape, dt, parts)

    P = Bump(0, 24 * 1024)
    ident_f = P([128], F32)
    ones_mean = P([128], BF16)
    ones_bf = P([128], BF16)
    epscol = P([1], F32)
    zerocol = P([1], F32)
    segmask = P([2 * nseg], F32)
    convflag = P([2 * NU], F32)
    relb = P([384], F32)
    nfg = P([8], F32)
    TAB = [P([8, 128], F32) for g in range(3)]
    pp = P([192], F32)
    vg_bc = P([1024], F32)
    bs_bc = P([8, 128], F32)
    cb = S.buf("const")
    ppb = S.buf("pp")
    tabb = S.buf("tab")
    NS, NB = 2, 5
    RB = Bump(P.limit, P.limit + NS * 8192 + NB * 4096)
    stg = [RB([2048], F32) for i in range(NS)]
    wbf = [RB([2048], BF16) for i in range(NB)]
    stgb = [S.buf(f"stg{i}") for i in range(NS)]
    wbb = [S.buf(f"wb{i}") for i in range(NB)]
    WBASE = RB.limit
    WLIM = TOTAL

    banks = [nc.alloc_psum_tensor(f"bank{i}", [128, 512], F32).ap() for i in range(8)]
    bankb = [S.buf(f"bank{i}") for i in range(8)]
    pbi = [0]

    def pb():
        i = pbi[0] % 8
        pbi[0] += 1
        return banks[i], bankb[i]

    dkc = {}

    def dk(name, n):
        i = dkc.get(name, 0)
        dkc[name] = i + 1
        return f"{name}{i % n}"

    def MM(out, lhsT, rhs, start, stop, reads, writes):
        S.op("pe", lambda e: e.matmul(out, lhsT=lhsT, rhs=rhs, start=start, stop=stop), reads, writes, nowaw=True)

    def TR(out, in_, reads, writes):
        S.op("pe", lambda e: e.transpose(out, in_, ident_f), list(reads) + [cb], writes, nowaw=True)

    def ACT(out, in_, func, reads, writes, bias=None, scale=1.0, nowaw=True):
        if bias is None:
            S.op("act", lambda e: e.activation(out=out, in_=in_, func=func, scale=scale), reads, writes, nowaw=nowaw)
        else:
            S.op("act", lambda e: e.activation(out=out, in_=in_, func=func, bias=bias, scale=scale), reads, writes, nowaw=nowaw)

    def TT(eng, out, in0, in1, op, reads, writes, nowaw=True):
        S.op(eng, lambda e: e.tensor_tensor(out=out, in0=in0, in1=in1, op=op), reads, writes, nowaw=nowaw)

    def TS(eng, out, in0, s1, s2, op0, op1, reads, writes, nowaw=True):
        if s2 is None:
            S.op(eng, lambda e: e.tensor_scalar(out=out, in0=in0, scalar1=s1, scalar2=None, op0=op0), reads, writes, nowaw=nowaw)
        else:
            S.op(eng, lambda e: e.tensor_scalar(out=out, in0=in0, scalar1=s1, scalar2=s2, op0=op0, op1=op1), reads, writes, nowaw=nowaw)

    def STT(eng, out, in0, scalar, in1, op0, op1, reads, writes, nowaw=True):
        S.op(eng, lambda e: e.scalar_tensor_tensor(out=out, in0=in0, scalar=scalar, in1=in1, op0=op0, op1=op1), reads, writes, nowaw=nowaw)

    def CP(eng, out, in_, reads, writes, nowaw=True):
        if eng == "act":
            ACT(out, in_, AF.Copy, reads, writes, nowaw=nowaw)
        else:
            S.op(eng, lambda e: e.tensor_copy(out=out, in_=in_), reads, writes, nowaw=nowaw)

    def RCP(out, in_, reads, writes):
        S.op("dve", lambda e: e.reciprocal(out=out, in_=in_), reads, writes, nowaw=True)

    def MEMSET(eng, ap, val, writes):
        S.op(eng, lambda e: e.memset(ap, val), (), writes, nowaw=True)

    def DMA(q, out, in_, reads, writes, sem, slow=False):
        if slow:
            S.dma(q, lambda e: e.dma_start(out=out, in_=in_, allow_slow_non_contiguous=True), reads, writes, sem=sem, nowaw=True)
        else:
            S.dma(q, lambda e: e.dma_start(out=out, in_=in_), reads, writes, sem=sem, nowaw=True)

    evi = [0]

    def EVAC(out, in_, reads, writes):
        evi[0] += 1
        CP("act" if evi[0] % 2 else "dve", out, in_, reads, writes)

    class WS:
        def __init__(self):
            self.plan = []
            self.loaded = 0
            self.released = set()
            self.kinds = {}
            self.seen = set()
            self.pending = []
            self.scr = None
            self.scrb = None

        def add(self, tag, src, p, a, b):
            kind = tag.split("_", 1)[1] if tag.startswith("in") else tag[0:2] + tag[tag.index("_"):]
            if kind not in self.kinds:
                self.kinds[kind] = len(self.kinds)
            self.plan.append((tag, src, p, a, b, self.kinds[kind]))

        def finalize(self):
            nk = len(self.kinds)
            self.scr = dscr("WSCR", [nk, 128, 2048], BF16)
            self.scrb = [S.buf(f"wscr{i}") for i in range(nk)]

        def _store(self, jj):
            tag, src, p, a, b, ks = self.plan[jj]
            n = a * b
            DMA("sp", self.scr[ks][0:p, 0:n], wbf[jj % NB][0:p, 0:n], [wbb[jj % NB]], [self.scrb[ks]], dk("wst", 2))

        def _flush(self, upto):
            while self.pending and self.pending[0] <= upto:
                self._store(self.pending.pop(0))

        def _load(self, j):
            tag, src, p, a, b, ks = self.plan[j]
            n = a * b
            if tag not in self.seen:
                self.seen.add(tag)
                self._flush(j - 2)
                sv = stg[j % NS][0:p, 0:n].rearrange("p (a b) -> p a b", a=a)
                DMA("sp", sv, src, (), [stgb[j % NS]], sem=f"w{j % NS}")
                S.op("pool", lambda e: e.tensor_copy(out=wbf[j % NB][0:p, 0:n], in_=stg[j % NS][0:p, 0:n]),
                     [stgb[j % NS]], [wbb[j % NB]])
                self.pending.append(j)
            else:
                self._flush(j)
                DMA("sp", wbf[j % NB][0:p, 0:n], self.scr[ks][0:p, 0:n], [self.scrb[ks]], [wbb[j % NB]], sem=f"wb{j % NB}")

        def _pump(self, upto):
            while self.loaded < len(self.plan) and self.loaded <= upto:
                j = self.loaded
                if j >= NB and (j - NB) not in self.released:
                    break
                self._load(j)
                self.loaded += 1

        def get(self, i, tag):
            assert self.plan[i][0] == tag, (i, self.plan[i][0], tag)
            self._pump(i + NB - 1)
            assert self.loaded > i, f"weight chunk {i} {tag} not loadable (ring deadlock)"
            tg, src, p, a, b, ks = self.plan[i]
            return wbf[i % NB][0:p, 0:a * b].rearrange("p (a b) -> p a b", a=a), wbb[i % NB]

        def release(self, i):
            self.released.add(i)
            self._pump(i + NB)

    ws = WS()
    wi = [0]

    def wget(tag):
        i = wi[0]
        wi[0] += 1
        v, b = ws.get(i, tag)
        return i, v, b

    def plan_layer_A(l):
        v = w_in[l].rearrange("(kc p) c -> p kc c", p=128)
        for j in (3, 4, 5, 6, 7, 8):
            ws.add(f"in{l}_{j}", v[:, :, 256 * j:256 * j + 256], 128, 8, 256)

    def plan_layer_B(l):
        v = w_in[l].rearrange("(kc p) c -> p kc c", p=128)
        for j in (0, 1, 2):
            ws.add(f"in{l}_{j}", v[:, :, 256 * j:256 * j + 256], 128, 8, 256)
        for j in (13, 14, 15, 16):
            ws.add(f"in{l}_{j}", v[:, :, 256 * j:256 * j + 256], 128, 8, 256)
        ws.add(f"ws{l}_0", w_sT[l].rearrange("p (a b) -> p a b", a=8), 128, 8, 128)
        for j in (9, 10, 11, 12):
            ws.add(f"in{l}_{j}", v[:, :, 256 * j:256 * j + 256], 128, 8, 256)
        vb = w_pb[l].rearrange("(kc p) c -> p kc c", p=128)
        for mp in range(4):
            ws.add(f"in{l}_{17 + mp}", v[:, :, 256 * (17 + mp):256 * (18 + mp)], 128, 8, 256)
            ws.add(f"in{l}_{21 + mp}", v[:, :, 256 * (21 + mp):256 * (22 + mp)], 128, 8, 256)
            ws.add(f"pa{l}_{mp}", w_pa[l][:, :, 256 * mp:256 * mp + 256], 64, 4, 256)
            ws.add(f"pb{l}_{mp}", vb[:, :, 256 * mp:256 * mp + 256], 128, 8, 256)
        vo = w_o[l].rearrange("(kc p) c -> p kc c", p=128)
        for j in range(4):
            ws.add(f"wo{l}_{j}", vo[:, :, 256 * j:256 * j + 256], 128, 8, 256)

    def plan_layer_C(l):
        vu = w_up[l].rearrange("(kc p) c -> p kc c", p=128)
        for j in range(NPAIR):
            ws.add(f"up{l}_{j}", vu[:, :, 256 * j:256 * j + 256], 128, 8, 256)
        vd = w_dn[l].rearrange("(kc p) c -> p kc c", p=128)
        for m in range(8):
            for hf in range(2):
                ws.add(f"dn{l}_{m}_{hf}", vd[:, 11 * hf:11 * hf + 11, 128 * m:128 * m + 128], 128, 11, 128)

    for l in range(depth):
        for s in range(nseg):
            plan_layer_A(l)
        for s in range(nseg):
            plan_layer_B(l)
        for u in range(NU):
            plan_layer_C(l)

    ws.finalize()
    prev_bufs = []

    pass_cache = {}
    last_kind = [None]

    def new_pass(kind=None):
        W = Bump(WBASE, WLIM)
        if kind is not None and kind == last_kind[0]:
            cache = pass_cache[kind]

            def mkc(name):
                return cache[name]
            return W, mkc
        ev = S.fence(prev_bufs)
        prev_bufs.clear()
        cache = {}
        pass_cache[kind] = cache
        last_kind[0] = kind

        def mk(name):
            b = S.buf(name, init=ev)
            prev_bufs.append(b)
            cache[name] = b
            return b
        return W, mk

    def ck(name):
        if stop == name:
            S.emit()
            raise _Stop(nc)

    DMA("sp", ident_f, ident_in, (), [cb], "c0")
    DMA("sp", segmask, segmask_in, (), [cb], "c1")
    DMA("sp", convflag, convflag_in, (), [cb], "c2")
    DMA("sp", relb, relb_in.partition_broadcast(128), (), [cb], "c3")
    DMA("sp", nfg, nf_in, (), [cb], "c4")
    MEMSET("dve", ones_mean, 1.0 / 1024.0, [cb])
    MEMSET("dve", ones_bf, 1.0, [cb])
    MEMSET("dve", epscol, EPS, [cb])
    MEMSET("dve", zerocol, 0.0, [cb])

    W, mk = new_pass()
    Z = W([6144], BF16)
    Zb = mk("Z")
    MEMSET("pool", Z, 0.0, [Zb])
    for c in range(6):
        DMA("sp", KT[c * 128:(c + 1) * 128, 0:PAD], Z[:, 0:PAD], [Zb], [KTp], dk("z", 4))
        DMA("sp", KT[c * 128:(c + 1) * 128, PAD + T:PAD + T + PAD], Z[:, 0:PAD], [Zb], [KTp], dk("z", 4))
    Z3 = Z.rearrange("p (a c) -> p a c", a=8)
    DMA("sp", VS[0:PAD, :].rearrange("(a p) c -> p a c", p=128), Z3, [Zb], [VSp], dk("z", 4))
    DMA("sp", VS[PAD + T:PAD + T + PAD, :].rearrange("(a p) c -> p a c", p=128), Z3, [Zb], [VSp], dk("z", 4))
    Zf = Z[:, 0:16].bitcast(F32).rearrange("p (a o) -> p a o", o=1)
    for Xv, hb in ((XAv, XAh), (XBv, XBh)):
        DMA("sp", Xv[:, :, 0:1], Zf, [Zb], [hb], dk("z", 4), slow=True)
        DMA("sp", Xv[:, :, T + 1:T + 2], Zf, [Zb], [hb], dk("z", 4), slow=True)
    EP = [W([128], F32) for i in range(2)]
    EPb = [mk(f"ep{i}") for i in range(2)]
    ei = 0
    for g in range(3):
        for ty in range(2):
            lst = _PDESC[g * 2 + ty]
            for n_, (b, pi) in enumerate(lst):
                DMA("sp", EP[ei % 2], epl_in[pi], (), [EPb[ei % 2]], f"ep{ei % 2}")
                for h in range(4):
                    sc = NEG if b < 0 else relb[:, b * 12 + 4 * g + h:b * 12 + 4 * g + h + 1]
                    if n_ == 0:
                        TS("dve", TAB[g][:, 2 * h + ty, :], EP[ei % 2], sc, None, ALU.mult, None, [EPb[ei % 2], cb], [tabb])
                    else:
                        STT("dve", TAB[g][:, 2 * h + ty, :], EP[ei % 2], sc, TAB[g][:, 2 * h + ty, :], ALU.mult, ALU.add,
                            [EPb[ei % 2], cb, tabb], [tabb])
                ei += 1

    ck("init")
    def rmsnorm(xt, xb, n, gain, gb, out, ob, SQ, SQb, RS, RSb):
        ACT(SQ[:, :, 0:n], xt, AF.Square, [xb], [SQb])
        bk, bb = pb()
        for kc in range(8):
            MM(bk[:, 0:n], ones_mean, SQ[:, kc, 0:n], kc == 0, kc == 7, [SQb, cb], [bb])
        ACT(RS[:, 0:n], bk[:, 0:n], AF.Sqrt, [bb, cb], [RSb], bias=epscol)
        RCP(RS[:, 0:n], RS[:, 0:n], [RSb], [RSb])
        for kc in range(8):
            STT("dve", out[:, kc, :], xt[:, kc, :], gain[:, kc:kc + 1], RS[:, 0:n],
                ALU.mult, ALU.mult, [xb, RSb, gb], [ob])

    XIN = [W([1024], F32) for i in range(2)]
    XINb = [mk(f"xin{i}") for i in range(2)]
    XT0 = [W([8, 128], F32) for i in range(2)]
    XT0b = [mk(f"xt0{i}") for i in range(2)]
    for tt in range(T // 128):
        i2 = tt % 2
        DMA("sp", XIN[i2], x_in[tt * 128:(tt + 1) * 128, :], (), [XINb[i2]], f"xin{i2}")
        for half in range(2):
            bk, bb = pb()
            for q in range(4):
                kc = half * 4 + q
                TR(bk[:, q * 128:(q + 1) * 128], XIN[i2][:, kc * 128:(kc + 1) * 128], [XINb[i2]], [bb])
            EVAC(XT0[i2][:, half * 4:half * 4 + 4, :], bk.rearrange("p (a b) -> p a b", a=4), [bb], [XT0b[i2]])
        DMA(STQ, XAv[:, :, 1 + tt * 128:1 + (tt + 1) * 128], XT0[i2], [XT0b[i2]], [XAb[tt // 4]], dk("st", 4))

    ck("p0")
    for l in range(depth):
        DMA("sp", pp, pp_in[l], (), [ppb], "pp0")
        DMA("sp", vg_bc, vg_in[l].partition_broadcast(128), (), [ppb], "pp1")
        DMA("sp", bs_bc.rearrange("p a b -> p (a b)"), bs_in[l].partition_broadcast(128), (), [ppb], "pp2")

        for s in range(nseg):
            W, mk = new_pass("A")
            XS = [W([8, 512], F32) for i in range(2)]
            XSb = [mk(f"xs{i}") for i in range(2)]
            SQ = W([8, 512], BF16)
            SQb = mk("sq")
            RS = W([512], F32)
            RSb = mk("rs")
            HT2 = [W([8, SEG], BF16) for i in range(2)]
            HT2b = [mk(f"hts{i}") for i in range(2)]
            HTs, HTsb = HT2[s % 2], HT2b[s % 2]
            KST = [W([SEG], BF16) for i in range(2)]
            KSTb = [mk(f"kst{i}") for i in range(2)]
            VST = [W([16, 256], BF16) for i in range(2)]
            VSTb = [mk(f"vst{i}") for i in range(2)]
            for t in range(4):
                gt = s * 4 + t
                DMA("sp", XS[t % 2], XAv[:, :, 1 + gt * 512:1 + (gt + 1) * 512], [XAb[gt]], [XSb[t % 2]], f"xs{t % 2}")
                rmsnorm(XS[t % 2], XSb[t % 2], 512, pp[:, 0:8], ppb, HTs[:, :, t * 512:(t + 1) * 512], HTsb, SQ, SQb, RS, RSb)
            DMA(STQ, HTv[:, :, s * SEG:(s + 1) * SEG], HTs, [HTsb], [HTb[s]], dk("st", 4))
            ki = 0
            for j in (3, 4, 5):
                ci, wv, wb_ = wget(f"in{l}_{j}")
                g = j - 3
                for m in range(2):
                    for t in range(4):
                        bk, bb = pb()
                        for kc in range(8):
                            MM(bk, wv[:, kc, m * 128:(m + 1) * 128], HTs[:, kc, t * 512:(t + 1) * 512], kc == 0, kc == 7, [wb_, HTsb], [bb])
                        EVAC(KST[ki % 2][:, t * 512:(t + 1) * 512], bk, [bb], [KSTb[ki % 2]])
                    r0 = g * 256 + m * 128
                    DMA(STQ, KT[r0:r0 + 128, PAD + s * SEG:PAD + (s + 1) * SEG], KST[ki % 2], [KSTb[ki % 2]], [KTb[s]], dk("st", 4))
                    ki += 1
                ws.release(ci)
            for j in (6, 7, 8):
                ci, wv, wb_ = wget(f"in{l}_{j}")
                g = j - 6
                vi = j % 2
                for t2 in range(8):
                    bk, bb = pb()
                    for q in range(2):
                        tt = t2 * 2 + q
                        for kc in range(8):
                            MM(bk[:, q * 256:(q + 1) * 256], HTs[:, kc, tt * 128:(tt + 1) * 128], wv[:, kc, :], kc == 0, kc == 7, [wb_, HTsb], [bb])
                    EVAC(VST[vi][:, t2 * 2:t2 * 2 + 2, :], bk.rearrange("p (a b) -> p a b", a=2), [bb], [VSTb[vi]])
                DMA(STQ, VS[PAD + s * SEG:PAD + (s + 1) * SEG, g * 256:(g + 1) * 256].rearrange("(tt p) c -> p tt c", p=128),
                    VST[vi], [VSTb[vi]], [VSb[s]], dk("st", 4))
                ws.release(ci)

        ck("A")
        for s in range(nseg):
            W, mk = new_pass(None)
            HTs = W([8, SEG], BF16)
            HTsb = mk("hts")
            OT = W([4, SEG], BF16)
            OTb = mk("ot")
            RA = W.o
            QT = W([6, SEG], BF16)
            QTb = mk("qt")
            KW = [W([4096], BF16) for i in range(2)]
            KWb = [mk(f"kw{i}") for i in range(2)]
            VW = [W([32, 128], BF16) for i in range(2)]
            VWb = [mk(f"vw{i}") for i in range(2)]
            ACN = W([2, SEG], F32)
            ACNb = mk("acn")
            ACD = W([2, SEG], F32)
            ACDb = mk("acd")
            TMP = [W([2, 128], F32) for i in range(4)]
            TMPb = [mk(f"tmp{i}") for i in range(4)]
            PT = [W([4, 128], BF16) for i in range(2)]
            PTb = [mk(f"pt{i}") for i in range(2)]
            RAend = W.o

            DMA("sp", HTs, HTv[:, :, s * SEG:(s + 1) * SEG], [HTb[s]], [HTsb], "hts")
            for j in (0, 1, 2):
                ci, wv, wb_ = wget(f"in{l}_{j}")
                for m in range(2):
                    for t in range(4):
                        bk, bb = pb()
                        for kc in range(8):
                            MM(bk, wv[:, kc, m * 128:(m + 1) * 128], HTs[:, kc, t * 512:(t + 1) * 512], kc == 0, kc == 7, [wb_, HTsb], [bb])
                        EVAC(QT[:, 2 * j + m, t * 512:(t + 1) * 512], bk, [bb], [QTb])
                ws.release(ci)
            ck("B1q")
            kvi = 0
            tpi = 0
            for pair in range(2):
                for g in range(3):
                    d = DIL[g]
                    nqt = 16 // d
                    Wn = SEG + 128 * d
                    kv = kvi % 2
                    kvi += 1
                    r0 = g * 256 + pair * 128
                    c0 = PAD + s * SEG - 64 * d
                    kdeps = [KTb[s], KTp] + ([KTb[s - 1]] if s > 0 else []) + ([KTb[s + 1]] if s + 1 < nseg else [])
                    vdeps = [VSb[s], VSp] + ([VSb[s - 1]] if s > 0 else []) + ([VSb[s + 1]] if s + 1 < nseg else [])
                    DMA("sp", KW[kv][:, 0:Wn], KT[r0:r0 + 128, c0:c0 + Wn], kdeps, [KWb[kv]], f"kw{kv}")
                    for r in range(d):
                        src = VS[c0 + r:c0 + r + ((nqt + 1) * 128 - 1) * d + 1:d, r0:r0 + 128].rearrange("(j p) c -> p j c", p=128)
                        DMA("sp", VW[kv][:, r * (nqt + 1):(r + 1) * (nqt + 1), :], src, vdeps, [VWb[kv]], dk("vw", 4))
                    ck("B1l")
                    for r in range(d):
                        for qt in range(nqt):
                            q0 = r + qt * 128 * d
                            qsl = slice(q0, q0 + 127 * d + 1, d)
                            pi_ = tpi % 2
                            bks = [pb(), pb()]
                            for h2 in range(2):
                                bk, bb = bks[h2]
                                for kt in range(2):
                                    k0 = (qt + kt) * 128 * d + r
                                    ksl = slice(k0, k0 + 127 * d + 1, d)
                                    MM(bk[:, kt * 128:(kt + 1) * 128], KW[kv][h2 * 64:(h2 + 1) * 64, ksl],
                                       QT[h2 * 64:(h2 + 1) * 64, 2 * g + pair, qsl], True, True, [KWb[kv], QTb], [bb])
                            for h2 in range(2):
                                bk, bb = bks[h2]
                                ti = (tpi * 2 + h2) % 4
                                hh = pair * 2 + h2
                                STT("dve", TMP[ti], bk[:, 0:256].rearrange("p (a b) -> p a b", a=2), 0.125,
                                    TAB[g][:, 2 * hh:2 * hh + 2, :], ALU.mult, ALU.add, [bb, tabb], [TMPb[ti]])
                                if qt != 0 and qt != nqt - 1:
                                    ACT(PT[pi_][:, 2 * h2:2 * h2 + 2, :], TMP[ti], AF.Exp, [TMPb[ti], cb], [PTb[pi_]], bias=zerocol)
                                else:
                                    for kt in range(2):
                                        if qt == 0 and kt == 0:
                                            bcol = segmask[:, 2 * s:2 * s + 1]
                                        elif qt == nqt - 1 and kt == 1:
                                            bcol = segmask[:, 2 * s + 1:2 * s + 2]
                                        else:
                                            bcol = zerocol
                                        ACT(PT[pi_][:, 2 * h2 + kt, :], TMP[ti][:, kt, :], AF.Exp, [TMPb[ti], cb], [PTb[pi_]], bias=bcol)
                            tpi += 1
                            bkn, bbn = pb()
                            for h2 in range(2):
                                for kt in range(2):
                                    vt = r * (nqt + 1) + qt + kt
                                    MM(bkn[0:64, h2 * 128:(h2 + 1) * 128], VW[kv][:, vt, h2 * 64:(h2 + 1) * 64], PT[pi_][:, 2 * h2 + kt, :],
                                       kt == 0, kt == 1, [VWb[kv], PTb[pi_]], [bbn])
                            for h2 in range(2):
                                for kt in range(2):
                                    MM(bkn[0:64, 256 + h2 * 128:256 + (h2 + 1) * 128], ones_bf[:, 0:64], PT[pi_][:, 2 * h2 + kt, :],
                                       kt == 0, kt == 1, [cb, PTb[pi_]], [bbn])
                            ck("B1m")
                            nv = bkn[0:64, 0:256].rearrange("p (a b) -> p a b", a=2)
                            dv = bkn[0:64, 256:512].rearrange("p (a b) -> p a b", a=2)
                            an = ACN[0:64, :, qsl]
                            ad = ACD[0:64, :, qsl]
                            if g == 0:
                                CP("act", an, nv, [bbn], [ACNb])
                                CP("act", ad, dv, [bbn], [ACDb])
                            else:
                                TT("dve", an, an, nv, ALU.add, [bbn, ACNb], [ACNb])
                                TT("pool" if False else "dve", ad, ad, dv, ALU.add, [bbn, ACDb], [ACDb])
                    ck("B1g")
                for t in range(4):
                    sl = slice(t * 512, (t + 1) * 512)
                    RCP(ACD[0:64, :, sl], ACD[0:64, :, sl], [ACDb], [ACDb])
                    TT("dve", OT[0:64, pair * 2:pair * 2 + 2, sl], ACN[0:64, :, sl], ACD[0:64, :, sl], ALU.mult, [ACNb, ACDb], [OTb])

            ck("B1")
            ev2 = S.fence([QTb, ACNb, ACDb] + KWb + VWb + TMPb + PTb)
            W2 = Bump(RA, WLIM)

            def mk2(name):
                b = S.buf(name, init=ev2)
                prev_bufs.append(b)
                return b
            VN = W2([16, 1024], BF16)
            VNb = mk2("vn")
            SG = W2([8, SEG], BF16)
            SGb = mk2("sg")
            GV = [W2([1024], F32) for i in range(2)]
            GVb = [mk2(f"gv{i}") for i in range(2)]
            SQV = W2([1024], BF16)
            SQVb = mk2("sqv")
            SS = [W2([1], F32) for i in range(2)]
            SSb = [mk2(f"ss{i}") for i in range(2)]
            UT = [W2([512], BF16) for i in range(2)]
            UTb = [mk2(f"ut{i}") for i in range(2)]
            ZT = [W2([512], F32) for i in range(2)]
            ZTb = [mk2(f"zt{i}") for i in range(2)]
            SGA = [W2([512], F32) for i in range(2)]
            SGAb = [mk2(f"sga{i}") for i in range(2)]
            SGB = [W2([512], F32) for i in range(2)]
            SGBb = [mk2(f"sgb{i}") for i in range(2)]
            T1 = [W2([512], F32) for i in range(2)]
            T1b = [mk2(f"t1{i}") for i in range(2)]
            T2 = [W2([512], F32) for i in range(2)]
            T2b = [mk2(f"t2{i}") for i in range(2)]
            vch = [wget(f"in{l}_{j}") for j in (13, 14, 15, 16)]
            for tt in range(16):
                i2 = tt % 2
                bks = [pb(), pb()]
                for c4 in range(4):
                    ci, wv, wb_ = vch[c4]
                    bk, bb = bks[c4 // 2]
                    for kc in range(8):
                        MM(bk[:, (c4 % 2) * 256:(c4 % 2 + 1) * 256], HTs[:, kc, tt * 128:(tt + 1) * 128], wv[:, kc, :], kc == 0, kc == 7,
                           [wb_, HTsb], [bb])
                for hb in range(2):
                    ACT(GV[i2][:, hb * 512:(hb + 1) * 512], bks[hb][0], AF.Gelu_apprx_tanh, [bks[hb][1]], [GVb[i2]])
                ACT(SQV, GV[i2], AF.Square, [GVb[i2]], [SQVb])
                S.op("dve", lambda e, o=SS[i2], i=SQV: e.reduce_sum(out=o, in_=i, axis=AX.X), [SQVb], [SSb[i2]], nowaw=True)
                ACT(SS[i2], SS[i2], AF.Sqrt, [SSb[i2], cb], [SSb[i2]], bias=epscol, scale=1.0 / 1024.0)
                RCP(SS[i2], SS[i2], [SSb[i2]], [SSb[i2]])
                STT("dve", VN[:, tt, :], GV[i2], SS[i2], vg_bc, ALU.mult, ALU.mult, [GVb[i2], SSb[i2], ppb], [VNb])
            for ci, wv, wb_ in vch:
                ws.release(ci)
            cws, wsv, wsb = wget(f"ws{l}_0")
            ui = 0
            for gp in range(4):
                ci, wv, wb_ = wget(f"in{l}_{9 + gp}")
                for gg in range(2):
                    g8 = gp * 2 + gg
                    for t in range(4):
                        sl = slice(t * 512, (t + 1) * 512)
                        i2 = ui % 2
                        ui += 1
                        bk, bb = pb()
                        for kc in range(8):
                            MM(bk, wv[:, kc, gg * 128:(gg + 1) * 128], HTs[:, kc, sl], kc == 0, kc == 7, [wb_, HTsb], [bb])
                        ACT(UT[i2], bk, AF.Gelu_apprx_tanh, [bb], [UTb[i2]])
                        bz, bzb = pb()
                        for n4 in range(4):
                            MM(bz[:, n4 * 128:(n4 + 1) * 128], VN[:, t * 4 + n4, g8 * 128:(g8 + 1) * 128], wsv[:, g8, :], True, True,
                               [VNb, wsb], [bzb])
                        TT("dve", ZT[i2].rearrange("p (a b) -> p a b", a=4), bz.rearrange("p (a b) -> p a b", a=4),
                           bs_bc[:, g8, :].unsqueeze(1).broadcast_to([128, 4, 128]), ALU.add, [bzb, ppb], [ZTb[i2]])
                        TT("pool", SG[:, g8, sl], ZT[i2], UT[i2], ALU.mult, [ZTb[i2], UTb[i2]], [SGb])
                ws.release(ci)
            ws.release(cws)
            MG = VN.rearrange("p a b -> p (a b)").rearrange("p (a b) -> p a b", a=8)
            MGb = S.buf("mg", init=S.fence([VNb]))
            prev_bufs.append(MGb)
            mi = 0
            for mp in range(4):
                cga, wga, bga = wget(f"in{l}_{17 + mp}")
                cgb, wgb, bgb = wget(f"in{l}_{21 + mp}")
                cpa, wpa, bpa = wget(f"pa{l}_{mp}")
                cpb, wpb, bpb = wget(f"pb{l}_{mp}")
                for mm in range(2):
                    m = mp * 2 + mm
                    ms = slice(mm * 128, (mm + 1) * 128)
                    for t in range(4):
                        sl = slice(t * 512, (t + 1) * 512)
                        i2 = mi % 2
                        mi += 1
                        bk, bb = pb()
                        for kc in range(8):
                            MM(bk, wga[:, kc, ms], HTs[:, kc, sl], kc == 0, kc == 7, [bga, HTsb], [bb])
                        ACT(SGA[i2], bk, AF.Sigmoid, [bb], [SGAb[i2]])
                        bk, bb = pb()
                        for kc in range(8):
                            MM(bk, wgb[:, kc, ms], HTs[:, kc, sl], kc == 0, kc == 7, [bgb, HTsb], [bb])
                        ACT(SGB[i2], bk, AF.Sigmoid, [bb], [SGBb[i2]])
                        bk, bb = pb()
                        for h in range(4):
                            MM(bk, wpa[0:64, h, ms], OT[0:64, h, sl], h == 0, h == 3, [bpa, OTb], [bb])
                        TT("dve", T1[i2], bk, SGA[i2], ALU.mult, [bb, SGAb[i2]], [T1b[i2]])
                        bk, bb = pb()
                        for kc in range(8):
                            MM(bk, wpb[:, kc, ms], SG[:, kc, sl], kc == 0, kc == 7, [bpb, SGb], [bb])
                        TT("dve", T2[i2], bk, SGB[i2], ALU.mult, [bb, SGBb[i2]], [T2b[i2]])
                        TT("pool", MG[:, m, sl], T1[i2], T2[i2], ALU.add, [T1b[i2], T2b[i2]], [MGb])
                for c_ in (cga, cgb, cpa, cpb):
                    ws.release(c_)
            evx = S.fence([HTsb])
            XS = [HTs.rearrange("p a b -> p (a b)")[:, i * 8192:(i + 1) * 8192].bitcast(F32).rearrange("p (a b) -> p a b", a=8) for i in range(2)]
            XSb = []
            for i in range(2):
                b_ = S.buf(f"xsB{i}", init=evx)
                prev_bufs.append(b_)
                XSb.append(b_)
            och = [wget(f"wo{l}_{j}") for j in range(4)]
            for t in range(4):
                gt = s * 4 + t
                i2 = t % 2
                sl = slice(t * 512, (t + 1) * 512)
                DMA("sp", XS[i2], XAv[:, :, 1 + gt * 512:1 + (gt + 1) * 512], [XAb[gt]], [XSb[i2]], f"xsB{i2}")
                for m in range(8):
                    ci, wv, wb_ = och[m // 2]
                    bk, bb = pb()
                    for kc in range(8):
                        MM(bk, wv[:, kc, (m % 2) * 128:(m % 2 + 1) * 128], MG[:, kc, sl], kc == 0, kc == 7, [wb_, MGb], [bb])
                    TT("dve", XS[i2][:, m, :], bk, XS[i2][:, m, :], ALU.add, [bb, XSb[i2]], [XSb[i2]])
                DMA(STQ, XBv[:, :, 1 + gt * 512:1 + (gt + 1) * 512], XS[i2], [XSb[i2]], [XBb[gt]], dk("st", 4))
            for ci, wv, wb_ in och:
                ws.release(ci)

        ck("B")
        last = (l == depth - 1)
        for u in range(NU):
            W, mk = new_pass("C")
            XU = [W([8, 512], F32) for i in range(2)]
            XUb = [mk(f"xu{i}") for i in range(2)]
            XH = W([8, 2], F32)
            XHb = mk("xh")
            SQ = W([8, 512], BF16)
            SQb = mk("sq")
            RS = W([512], F32)
            RSb = mk("rs")
            H2 = W([8, 1026], BF16)
            H2b = mk("h2")
            GOFF = W.o
            G = W([NPAIR, UNIT], BF16)
            Gb = mk("g")
            RC = W.o
            AG = [W([1026], F32) for i in range(2)]
            AGb = [mk(f"ag{i}") for i in range(2)]
            AV = [W([1026], F32) for i in range(2)]
            AVb = [mk(f"av{i}") for i in range(2)]
            CG = [W([UNIT], F32) for i in range(2)]
            CGb = [mk(f"cg{i}") for i in range(2)]
            CV = [W([UNIT], F32) for i in range(2)]
            CVb = [mk(f"cv{i}") for i in range(2)]
            HH = W([8, 2], F32)
            HHb = mk("hh")

            def c_load(u_):
                c0 = u_ * UNIT
                for t in range(2):
                    gt = u_ * 2 + t
                    DMA("sp", XU[t], XBv[:, :, 1 + gt * 512:1 + (gt + 1) * 512], [XBb[gt]], [XUb[t]], f"xu{t}")
                hdeps = [XBh] + ([XBb[u_ * 2 - 1]] if u_ > 0 else []) + ([XBb[u_ * 2 + 2]] if u_ * 2 + 2 < T // 512 else [])
                DMA("sp", XH[:, :, 0:1], XBv[:, :, c0:c0 + 1], hdeps, [XHb], "xh0", slow=True)
                DMA("sp", XH[:, :, 1:2], XBv[:, :, c0 + UNIT + 1:c0 + UNIT + 2], hdeps, [XHb], "xh1", slow=True)

            def c_norm(u_):
                for t in range(2):
                    rmsnorm(XU[t], XUb[t], 512, pp[:, 8:16], ppb, H2[:, :, 1 + t * 512:1 + (t + 1) * 512], H2b, SQ, SQb, RS, RSb)
                rmsnorm(XH, XHb, 2, pp[:, 8:16], ppb, HH, HHb, SQ, SQb, RS, RSb)
                for side in range(2):
                    col = 0 if side == 0 else 1025
                    TS("dve", H2[:, :, col:col + 1], HH[:, :, side:side + 1], convflag[:, 2 * u_ + side:2 * u_ + side + 1], None, ALU.mult, None,
                       [HHb, cb], [H2b])

            if u == 0:
                c_load(0)
                c_norm(0)
            pieces = ((0, 512), (512, 512), (1024, 2))
            for j in range(NPAIR):
                if j == 1 and u + 1 < NU:
                    c_load(u + 1)
                ci, wv, wb_ = wget(f"up{l}_{j}")
                i2 = j % 2
                for m in range(2):
                    A_, Ab_ = (AG[i2], AGb[i2]) if m == 0 else (AV[i2], AVb[i2])
                    for (o0, n) in pieces:
                        bk, bb = pb()
                        for kc in range(8):
                            MM(bk[:, 0:n], wv[:, kc, m * 128:(m + 1) * 128], H2[:, kc, o0:o0 + n], kc == 0, kc == 7, [wb_, H2b], [bb])
                        CP("act", A_[:, o0:o0 + n], bk[:, 0:n], [bb], [Ab_])
                        C_, Cb_ = (CG[i2], CGb[i2]) if m == 0 else (CV[i2], CVb[i2])
                        cc = 2 * j + m
                        lo = 1 if o0 == 0 else 0
                        hi = min(o0 + n, 1025) - o0
                        S.op("act", lambda e, o=C_[:, o0 + lo - 1:o0 + hi - 1], i=bk[:, lo:hi], sc=pp[:, 60 + cc:61 + cc], bi=pp[:, 148 + cc:149 + cc]:
                             e.activation(out=o, in_=i, func=AF.Identity, bias=bi, scale=sc), [bb, ppb], [Cb_], nowaw=True)
                ws.release(ci)
                for m in range(2):
                    A_, Ab_ = (AG[i2], AGb[i2]) if m == 0 else (AV[i2], AVb[i2])
                    C_, Cb_ = (CG[i2], CGb[i2]) if m == 0 else (CV[i2], CVb[i2])
                    eng = "dve"
                    cc = 2 * j + m
                    w0 = pp[:, 16 + cc:17 + cc]
                    w1 = pp[:, 60 + cc:61 + cc]
                    w2 = pp[:, 104 + cc:105 + cc]
                    bcv = pp[:, 148 + cc:149 + cc]
                    STT(eng, C_, A_[:, 0:1024], w0, C_, ALU.mult, ALU.add, [Ab_, ppb, Cb_], [Cb_])
                    STT(eng, C_, A_[:, 2:1026], w2, C_, ALU.mult, ALU.add, [Ab_, ppb, Cb_], [Cb_])
                ACT(CG[i2], CG[i2], AF.Gelu_apprx_tanh, [CGb[i2]], [CGb[i2]])
                TT("dve", G[:, j, :], CG[i2], CV[i2], ALU.mult, [CGb[i2], CVb[i2]], [Gb])
            rcb = AGb + AVb + CGb + CVb
            evr = S.fence(rcb)
            WR = Bump(RC, WLIM)
            XR = [WR([8, 512], F32) for i in range(2)]
            XRb = [S.buf(f"xr{i}", init=evr) for i in range(2)]
            for t in range(2):
                gt = u * 2 + t
                DMA("sp", XR[t], XBv[:, :, 1 + gt * 512:1 + (gt + 1) * 512], [XBb[gt]], [XRb[t]], f"xr{t}")
            for m in range(8):
                if m == 4 and u + 1 < NU:
                    c_norm(u + 1)
                bks = [pb(), pb()]
                for hf in range(2):
                    ci, wv, wb_ = wget(f"dn{l}_{m}_{hf}")
                    for t in range(2):
                        bk, bb = bks[t]
                        for k11 in range(11):
                            kc = hf * 11 + k11
                            MM(bk, wv[:, k11, :], G[:, kc, t * 512:(t + 1) * 512], kc == 0, kc == 21, [wb_, Gb], [bb])
                    ws.release(ci)
                for t in range(2):
                    TT("dve", XR[t][:, m, :], bks[t][0], XR[t][:, m, :], ALU.add, [bks[t][1], XRb[t]], [XRb[t]])
            if not last:
                for t in range(2):
                    gt = u * 2 + t
                    DMA(STQ, XAv[:, :, 1 + gt * 512:1 + (gt + 1) * 512], XR[t], [XRb[t]], [XAb[gt]], dk("st", 4))
                S.absorb(rcb, XRb)
            else:
                evy = S.fence([Gb])
                WY = Bump(GOFF, WLIM)
                YT = WY([8, 512], F32)
                YTb = S.buf("yt", init=evy)
                YO = [WY([1024], F32) for i in range(2)]
                YOb = [S.buf(f"yo{i}", init=evy) for i in range(2)]
                for t in range(2):
                    gt = u * 2 + t
                    rmsnorm(XR[t], XRb[t], 512, nfg, cb, YT, YTb, SQ, SQb, RS, RSb)
                    for q in range(4):
                        tt = gt * 4 + q
                        i2 = q % 2
                        for half in range(2):
                            bk, bb = pb()
                            for c4 in range(4):
                                kc = half * 4 + c4
                                TR(bk[:, c4 * 128:(c4 + 1) * 128], YT[:, kc, q * 128:(q + 1) * 128], [YTb], [bb])
                            EVAC(YO[i2][:, half * 512:(half + 1) * 512], bk, [bb], [YOb[i2]])
                        DMA(STQ, y_out[tt * 128:(tt + 1) * 128, :], YO[i2], [YOb[i2]], (), dk("yst", 4))
                S.absorb(rcb, XRb)
                S.absorb([Gb], [YTb] + YOb)

    assert wi[0] == len(ws.plan), (wi[0], len(ws.plan))
    S.emit()
    return nc


NSEG_FULL = 6
DEPTH_FULL = 4


def _core_layout():
    cores = []
    for c in range(8):
        segs = []
        if c < 4:
            for k in range(4):
                segs.append(("p", c, k * SEG))
            segs.append(("s", 2 * c, 0))
            segs.append(("s", 2 * c + 1, 0))
        else:
            for k in range(6):
                segs.append(("s", 8 + 6 * (c - 4) + k, 0))
        cores.append(segs)
    return cores


def _flags(segs):
    n = len(segs)
    soft_start = []
    for i, sg in enumerate(segs):
        if i > 0 and sg[0] == "p" and segs[i - 1][0] == "p" and segs[i - 1][1] == sg[1] and segs[i - 1][2] + SEG == sg[2]:
            soft_start.append(True)
        else:
            soft_start.append(False)
    soft_end = [(i + 1 < n and soft_start[i + 1]) for i in range(n)]
    segmask = np.zeros((128, 2 * n), np.float32)
    nu = n * (SEG // UNIT)
    convflag = np.zeros((128, 2 * nu), np.float32)
    for i in range(n):
        if not soft_start[i]:
            segmask[0:64, 2 * i] = NEG
        if not soft_end[i]:
            segmask[64:128, 2 * i + 1] = NEG
        for uu in range(SEG // UNIT):
            u = i * (SEG // UNIT) + uu
            st = soft_start[i] if uu == 0 else True
            en = soft_end[i] if uu == SEG // UNIT - 1 else True
            convflag[:, 2 * u] = 1.0 if st else 0.0
            convflag[:, 2 * u + 1] = 1.0 if en else 0.0
    return segmask, convflag


def _prep_shared(depth, rel_bias, norm_mix, w_in, v_gain, w_s, b_s, w_proj_a, w_proj_b, w_out,
                 norm_ffn, w_up, conv_w, conv_b, w_down, norm_final):
    f = lambda a: np.ascontiguousarray(np.asarray(a, dtype=np.float32))
    L = depth
    sh = {}
    sh["relb"] = f(rel_bias).reshape(1, 384)
    sh["epl"] = _PLANES
    sh["ident"] = np.eye(128, dtype=np.float32)
    sh["w_in"] = f(w_in[:L])
    sh["w_sT"] = f(np.transpose(np.asarray(w_s[:L]), (0, 3, 1, 2)).reshape(L, 128, 1024))
    sh["w_pa"] = f(np.asarray(w_proj_a[:L]).reshape(L, 4, 64, 1024).transpose(0, 2, 1, 3))
    sh["w_pb"] = f(w_proj_b[:L])
    sh["w_o"] = f(w_out[:L])
    wu = np.asarray(w_up[:L]).reshape(L, D, 2, NPAIR, 128).transpose(0, 1, 3, 2, 4).reshape(L, D, 2 * DFF)
    sh["w_up"] = f(wu)
    sh["w_dn"] = f(w_down[:L])
    pp = np.zeros((L, 128, 192), np.float32)
    pp[:, :, 0:8] = np.asarray(norm_mix[:L]).reshape(L, 8, 128).transpose(0, 2, 1)
    pp[:, :, 8:16] = np.asarray(norm_ffn[:L]).reshape(L, 8, 128).transpose(0, 2, 1)
    cw = np.asarray(conv_w[:L]).reshape(L, 3, 2, NPAIR, 128).transpose(0, 1, 4, 3, 2).reshape(L, 3, 128, 44)
    for tap in range(3):
        pp[:, :, 16 + 44 * tap:16 + 44 * (tap + 1)] = cw[:, tap]
    cbv = np.asarray(conv_b[:L]).reshape(L, 2, NPAIR, 128).transpose(0, 3, 2, 1).reshape(L, 128, 44)
    pp[:, :, 148:192] = cbv
    sh["pp"] = pp
    sh["nf"] = f(np.asarray(norm_final).reshape(8, 128).T)
    sh["vg"] = f(np.asarray(v_gain[:L]).reshape(L, 1, 1024))
    sh["bs"] = f(np.asarray(b_s[:L]).reshape(L, 1, 1024))
    return sh


_NC_CACHE = {}


def kernel(x_prompt, x_sample, rel_bias, norm_mix, w_in, v_gain, w_s, b_s, w_proj_a, w_proj_b,
           w_out, norm_ffn, w_up, conv_w, conv_b, w_down, norm_final):
    xp = np.asarray(x_prompt, dtype=np.float32)
    xs = np.asarray(x_sample, dtype=np.float32)
    sh = _prep_shared(DEPTH_FULL, rel_bias, norm_mix, w_in, v_gain, w_s, b_s, w_proj_a, w_proj_b, w_out,
                      norm_ffn, w_up, conv_w, conv_b, w_down, norm_final)
    cores = _core_layout()
    in_maps = []
    for segs in cores:
        xc = np.concatenate([(xp if w == "p" else xs)[b, st:st + SEG] for (w, b, st) in segs], axis=0)
        sm, cf = _flags(segs)
        m = dict(sh)
        m["x"] = np.ascontiguousarray(xc)
        m["segmask"] = sm
        m["convflag"] = cf
        in_maps.append(m)
    key = (NSEG_FULL, DEPTH_FULL)
    if key not in _NC_CACHE:
        _NC_CACHE[key] = build(*key)
    nc = _NC_CACHE[key]
    res = run_bass_kernel_spmd(nc, in_maps, core_ids=list(range(8)))
    yp = np.zeros_like(xp)
    ys = np.zeros_like(xs)
    for c, segs in enumerate(cores):
        yc = res.results[c]["y"]
        for i, (w, b, st) in enumerate(segs):
            (yp if w == "p" else ys)[b, st:st + SEG] = yc[i * SEG:(i + 1) * SEG]
    return (yp, ys)
```

```python
import contextlib
import os
_SKIP = os.environ.get('KSKIP', '')
import math
import numpy as np
import concourse.bass as bass
import concourse.mybir as mybir
from concourse.bass_utils import run_bass_kernel_spmd

F32 = mybir.dt.float32
BF16 = mybir.dt.bfloat16
AF = mybir.ActivationFunctionType
ALU = mybir.AluOpType
AX = mybir.AxisListType

D = 1024
SEG = 2048
PAD = 1024
UNIT = 1024
DIL = (1, 4, 16)
NEG = -30000.0
EPS = 1e-6
DFF = 2816
NPAIR = 22
STQ = "pool"


class Buf:
    __slots__ = ("name", "w", "r", "pr")

    def __init__(self, name="", init=None):
        self.name = name
        self.w = {}
        self.r = dict(init) if init else {}
        self.pr = {}


def _merge(d, k, v):
    if d.get(k, 0) < v:
        d[k] = v


class Sched:
    CH = 8000
    ENGS = ("pe", "act", "dve", "pool", "sp")

    def __init__(self, nc, same_engine_sync=True):
        self.nc = nc
        self.ops = {e: [] for e in self.ENGS}
        self.count = {e: 0 for e in self.ENGS}
        self.waited = {}
        self.maxep = {}
        self.dcum = {}
        self.same = same_engine_sync
        self.nwait = 0
        self.allbufs = []

    def buf(self, name="", init=None):
        b = Buf(name, init)
        self.allbufs.append(b)
        return b

    def fence(self, bufs):
        ev = {}
        for b in bufs:
            for k, v in b.w.items():
                _merge(ev, k, v)
            for k, v in b.r.items():
                _merge(ev, k, v)
            for k, v in b.pr.items():
                _merge(ev, k, v)
        return ev

    def absorb(self, targets, sources):
        ev = self.fence(sources)
        for t in targets:
            for k, v in ev.items():
                _merge(t.r, k, v)

    def _deps(self, eng, reads, writes, nowaw):
        deps = {}
        for b in reads:
            for k, v in b.w.items():
                _merge(deps, k, v)
        for b in writes:
            for k, v in b.r.items():
                _merge(deps, k, v)
            for k, v in b.pr.items():
                _merge(deps, k, v)
            if not nowaw:
                for k, v in b.w.items():
                    _merge(deps, k, v)
        waits = []
        for k, v in deps.items():
            if k[0] == "e":
                if k[1] == eng and (eng == "pe" or not self.same):
                    continue
                if self.maxep.get((eng, k[1]), -1) > k[2]:
                    continue
            if self.waited.get((eng, k), 0) >= v:
                continue
            self.waited[(eng, k)] = v
            if k[0] == "e":
                self.maxep[(eng, k[1])] = max(self.maxep.get((eng, k[1]), -1), k[2])
            waits.append((k, v))
        return waits

    def _update(self, ev, reads, writes, nowaw):
        k, v = ev
        for b in reads:
            _merge(b.r, k, v)
        for b in writes:
            if b.r or not nowaw:
                b.w = {k: v}
                if b.r:
                    b.pr = b.r
                b.r = {}
            else:
                _merge(b.w, k, v)

    def op(self, eng, fn, reads=(), writes=(), nowaw=False):
        waits = self._deps(eng, reads, writes, nowaw)
        n = self.count[eng]
        self.count[eng] = n + 1
        ev = (("e", eng, n // self.CH), n % self.CH + 1)
        self.ops[eng].append((waits, fn, ev[0], 1))
        self._update(ev, reads, writes, nowaw)
        self.nwait += len(waits)

    def dma(self, q, fn, reads=(), writes=(), sem=None, nowaw=False):
        key = ("d", sem)
        waits = self._deps(q, reads, writes, nowaw)
        prev = self.dcum.get(key, 0)
        if prev and self.waited.get((q, key), 0) < prev:
            self.waited[(q, key)] = prev
            waits.append((key, prev))
        cur = prev + 16
        assert cur < 60000, f"dma sem overflow {sem}"
        self.dcum[key] = cur
        self.ops[q].append((waits, fn, key, 16))
        self._update((key, cur), reads, writes, nowaw)
        self.nwait += len(waits)

    def emit(self):
        nc = self.nc
        keys = set()
        for e in self.ENGS:
            for waits, fn, k, inc in self.ops[e]:
                keys.add(k)
        keys = sorted(keys, key=str)
        print(f"[sched] ops={ {e: len(self.ops[e]) for e in self.ENGS} } waits={self.nwait} sems={len(keys)}", flush=True)
        with contextlib.ExitStack() as st:
            sems = {}
            for i, k in enumerate(keys):
                sems[k] = st.enter_context(nc.semaphore(f"s{i}"))
            block = st.enter_context(nc.Block())
            finals = list(self.dcum.items())

            def replay(eng, e):
                for waits, fn, k, inc in self.ops[eng]:
                    for wk, wv in waits:
                        e.wait_ge(sems[wk], wv)
                    fn(e).then_inc(sems[k], inc)
                if eng == "sp":
                    for k, v in finals:
                        e.wait_ge(sems[k], v)

            @block.tensor
            def _(e):
                replay("pe", e)

            @block.scalar
            def _(e):
                replay("act", e)

            @block.vector
            def _(e):
                replay("dve", e)

            @block.gpsimd
            def _(e):
                replay("pool", e)

            @block.sync
            def _(e):
                replay("sp", e)


def _t5_bucket(rel):
    nb = 16
    max_exact = 8
    ret = np.where(rel > 0, nb, 0)
    n = np.abs(rel)
    nf = np.maximum(n, 1).astype(np.float32)
    large = max_exact + (np.log(nf / max_exact) / math.log(1024 / max_exact) * (nb - max_exact)).astype(np.int32)
    large = np.minimum(large, nb - 1)
    return (ret + np.where(n < max_exact, n, large)).astype(np.int32)


def _planes():
    planes = []
    desc = []
    p = np.arange(128)[:, None]
    n = np.arange(128)[None, :]
    for g, d in enumerate(DIL):
        for ty, c0 in enumerate((-64, 64)):
            rel = p - n + c0
            band = np.abs(rel) <= 64
            bk = _t5_bucket(rel * d)
            lst = []
            for b in range(32):
                m = band & (bk == b)
                if m.any():
                    lst.append((b, len(planes)))
                    planes.append(m.astype(np.float32))
            lst.append((-1, len(planes)))
            planes.append((~band).astype(np.float32))
            desc.append(lst)
    return np.stack(planes), desc


_PLANES, _PDESC = _planes()


class _Stop(Exception):
    pass


def build(nseg, depth, debug=False, stop=None):
    try:
        return _build(nseg, depth, debug, stop)
    except _Stop as e:
        return e.args[0]


def _build(nseg, depth, debug=False, stop=None):
    T = nseg * SEG
    NU = T // UNIT
    nc = bass.Bass("TRN2", target_bir_lowering=False)
    S = Sched(nc)

    def din(name, shape, dt=F32):
        return nc.dram_tensor(name, list(shape), dt, kind="ExternalInput").ap()

    x_in = din("x", [T, D])
    segmask_in = din("segmask", [128, 2 * nseg])
    convflag_in = din("convflag", [128, 2 * NU])
    relb_in = din("relb", [1, 384])
    epl_in = din("epl", list(_PLANES.shape))
    ident_in = din("ident", [128, 128])
    w_in = din("w_in", [depth, D, 6400])
    w_sT = din("w_sT", [depth, 128, 1024])
    w_pa = din("w_pa", [depth, 64, 4, 1024])
    w_pb = din("w_pb", [depth, D, D])
    w_o = din("w_o", [depth, D, D])
    w_up = din("w_up", [depth, D, 2 * DFF])
    w_dn = din("w_dn", [depth, DFF, D])
    pp_in = din("pp", [depth, 128, 192])
    nf_in = din("nf", [128, 8])
    vg_in = din("vg", [depth, 1, 1024])
    bs_in = din("bs", [depth, 1, 1024])
    y_out = nc.dram_tensor("y", [T, D], F32, kind="ExternalOutput").ap()

    def dscr(name, shape, dt):
        return nc.dram_tensor(name, list(shape), dt, kind="ExternalOutput" if debug else "Internal").ap()

    XA = dscr("XA", [D, T + 2], F32)
    XB = dscr("XB", [D, T + 2], F32)
    HT = dscr("HT", [D, T], BF16)
    KT = dscr("KT", [768, PAD + T + PAD], BF16)
    VS = dscr("VS", [PAD + T + PAD, 768], BF16)
    XAv = XA.rearrange("(kc p) t -> p kc t", p=128)
    XBv = XB.rearrange("(kc p) t -> p kc t", p=128)
    HTv = HT.rearrange("(kc p) t -> p kc t", p=128)
    XAb = [S.buf(f"XA{i}") for i in range(T // 512)]
    XBb = [S.buf(f"XB{i}") for i in range(T // 512)]
    XAh, XBh = S.buf("XAh"), S.buf("XBh")
    HTb = [S.buf(f"HT{i}") for i in range(nseg)]
    KTb = [S.buf(f"KT{i}") for i in range(nseg)]
    VSb = [S.buf(f"VS{i}") for i in range(nseg)]
    KTp, VSp = S.buf("KTp"), S.buf("VSp")

    TOTAL = 207 * 1024
    M = nc.alloc_sbuf_tensor("M", [128, TOTAL // 2], BF16).ap()

    def view(off, shape, dt, parts=128):
        n = int(np.prod(shape))
        nb = n * (4 if dt == F32 else 2)
        assert off % 4 == 0
        v = M[0:parts, off // 2:(off + nb) // 2]
        if dt == F32:
            v = v.bitcast(F32)
        if len(shape) == 2:
            v = v.rearrange("p (a b) -> p a b", a=shape[0])
        elif len(shape) == 3:
            v = v.rearrange("p (a b c) -> p a b c", a=shape[0], b=shape[1])
        return v

    class Bump:
        def __init__(self, base, limit):
            self.o = base
            self.limit = limit

        def __call__(self, shape, dt, parts=128):
            n = int(np.prod(shape)) * (4 if dt == F32 else 2)
            n = (n + 31) // 32 * 32
            off = self.o
            self.o += n
            assert self.o <= self.limit, f"sbuf overflow {self.o} > {self.limit}"
            return view(off, shape, dt, parts)

    P = Bump(0, 24 * 1024)
    ident_f = P([128], F32)
    ones_mean = P([128], BF16)
    ones_bf = P([128], BF16)
    epscol = P([1], F32)
    zerocol = P([1], F32)
    segmask = P([2 * nseg], F32)
    convflag = P([2 * NU], F32)
    relb = P([384], F32)
    nfg = P([8], F32)
    TAB = [P([8, 128], F32) for g in range(3)]
    pp = P([192], F32)
    vg_bc = P([1024], F32)
    bs_bc = P([8, 128], F32)
    cb = S.buf("const")
    ppb = S.buf("pp")
    tabb = S.buf("tab")
    NS, NB = 2, 5
    RB = Bump(P.limit, P.limit + NS * 8192 + NB * 4096)
    stg = [RB([2048], F32) for i in range(NS)]
    wbf = [RB([2048], BF16) for i in range(NB)]
    stgb = [S.buf(f"stg{i}") for i in range(NS)]
    wbb = [S.buf(f"wb{i}") for i in range(NB)]
    WBASE = RB.limit
    WLIM = TOTAL

    banks = [nc.alloc_psum_tensor(f"bank{i}", [128, 512], F32).ap() for i in range(8)]
    bankb = [S.buf(f"bank{i}") for i in range(8)]
    pbi = [0]

    def pb():
        i = pbi[0] % 8
        pbi[0] += 1
        return banks[i], bankb[i]

    dkc = {}

    def dk(name, n):
        i = dkc.get(name, 0)
        dkc[name] = i + 1
        return f"{name}{i % n}"

    def MM(out, lhsT, rhs, start, stop, reads, writes):
        S.op("pe", lambda e: e.matmul(out, lhsT=lhsT, rhs=rhs, start=start, stop=stop), reads, writes, nowaw=True)

    def TR(out, in_, reads, writes):
        S.op("pe", lambda e: e.transpose(out, in_, ident_f), list(reads) + [cb], writes, nowaw=True)

    def ACT(out, in_, func, reads, writes, bias=None, scale=1.0, nowaw=True):
        if bias is None:
            S.op("act", lambda e: e.activation(out=out, in_=in_, func=func, scale=scale), reads, writes, nowaw=nowaw)
        else:
            S.op("act", lambda e: e.activation(out=out, in_=in_, func=func, bias=bias, scale=scale), reads, writes, nowaw=nowaw)

    def TT(eng, out, in0, in1, op, reads, writes, nowaw=True):
        S.op(eng, lambda e: e.tensor_tensor(out=out, in0=in0, in1=in1, op=op), reads, writes, nowaw=nowaw)

    def TS(eng, out, in0, s1, s2, op0, op1, reads, writes, nowaw=True):
        if s2 is None:
            S.op(eng, lambda e: e.tensor_scalar(out=out, in0=in0, scalar1=s1, scalar2=None, op0=op0), reads, writes, nowaw=nowaw)
        else:
            S.op(eng, lambda e: e.tensor_scalar(out=out, in0=in0, scalar1=s1, scalar2=s2, op0=op0, op1=op1), reads, writes, nowaw=nowaw)

    def STT(eng, out, in0, scalar, in1, op0, op1, reads, writes, nowaw=True):
        S.op(eng, lambda e: e.scalar_tensor_tensor(out=out, in0=in0, scalar=scalar, in1=in1, op0=op0, op1=op1), reads, writes, nowaw=nowaw)

    def CP(eng, out, in_, reads, writes, nowaw=True):
        if eng == "act":
            ACT(out, in_, AF.Copy, reads, writes, nowaw=nowaw)
        else:
            S.op(eng, lambda e: e.tensor_copy(out=out, in_=in_), reads, writes, nowaw=nowaw)

    def RCP(out, in_, reads, writes):
        S.op("dve", lambda e: e.reciprocal(out=out, in_=in_), reads, writes, nowaw=True)

    def MEMSET(eng, ap, val, writes):
        S.op(eng, lambda e: e.memset(ap, val), (), writes, nowaw=True)

    def DMA(q, out, in_, reads, writes, sem, slow=False):
        if slow:
            S.dma(q, lambda e: e.dma_start(out=out, in_=in_, allow_slow_non_contiguous=True), reads, writes, sem=sem, nowaw=True)
        else:
            S.dma(q, lambda e: e.dma_start(out=out, in_=in_), reads, writes, sem=sem, nowaw=True)

    evi = [0]

    def EVAC(out, in_, reads, writes):
        evi[0] += 1
        CP("act" if evi[0] % 2 else "dve", out, in_, reads, writes)

    class WS:
        def __init__(self):
            self.plan = []
            self.loaded = 0
            self.released = set()
            self.kinds = {}
            self.seen = set()
            self.pending = []
            self.scr = None
            self.scrb = None

        def add(self, tag, src, p, a, b):
            kind = tag.split("_", 1)[1] if tag.startswith("in") else tag[0:2] + tag[tag.index("_"):]
            if kind not in self.kinds:
                self.kinds[kind] = len(self.kinds)
            self.plan.append((tag, src, p, a, b, self.kinds[kind]))

        def finalize(self):
            nk = len(self.kinds)
            self.scr = dscr("WSCR", [nk, 128, 2048], BF16)
            self.scrb = [S.buf(f"wscr{i}") for i in range(nk)]

        def _store(self, jj):
            tag, src, p, a, b, ks = self.plan[jj]
            n = a * b
            DMA("sp", self.scr[ks][0:p, 0:n], wbf[jj % NB][0:p, 0:n], [wbb[jj % NB]], [self.scrb[ks]], dk("wst", 2))

        def _flush(self, upto):
            while self.pending and self.pending[0] <= upto:
                self._store(self.pending.pop(0))

        def _load(self, j):
            tag, src, p, a, b, ks = self.plan[j]
            n = a * b
            if tag not in self.seen:
                self.seen.add(tag)
                self._flush(j - 2)
                sv = stg[j % NS][0:p, 0:n].rearrange("p (a b) -> p a b", a=a)
                DMA("sp", sv, src, (), [stgb[j % NS]], sem=f"w{j % NS}")
                S.op("pool", lambda e: e.tensor_copy(out=wbf[j % NB][0:p, 0:n], in_=stg[j % NS][0:p, 0:n]),
                     [stgb[j % NS]], [wbb[j % NB]])
                self.pending.append(j)
            else:
                self._flush(j)
                DMA("sp", wbf[j % NB][0:p, 0:n], self.scr[ks][0:p, 0:n], [self.scrb[ks]], [wbb[j % NB]], sem=f"wb{j % NB}")

        def _pump(self, upto):
            while self.loaded < len(self.plan) and self.loaded <= upto:
                j = self.loaded
                if j >= NB and (j - NB) not in self.released:
                    break
                self._load(j)
                self.loaded += 1

        def get(self, i, tag):
            assert self.plan[i][0] == tag, (i, self.plan[i][0], tag)
            self._pump(i + NB - 1)
            assert self.loaded > i, f"weight chunk {i} {tag} not loadable (ring deadlock)"
            tg, src, p, a, b, ks = self.plan[i]
            return wbf[i % NB][0:p, 0:a * b].rearrange("p (a b) -> p a b", a=a), wbb[i % NB]

        def release(self, i):
            self.released.add(i)
            self._pump(i + NB)

    ws = WS()
    wi = [0]

    def wget(tag):
        i = wi[0]
        wi[0] += 1
        v, b = ws.get(i, tag)
        return i, v, b

    def plan_layer_A(l):
        v = w_in[l].rearrange("(kc p) c -> p kc c", p=128)
        for j in (3, 4, 5, 6, 7, 8):
            ws.add(f"in{l}_{j}", v[:, :, 256 * j:256 * j + 256], 128, 8, 256)

    def plan_layer_B(l):
        v = w_in[l].rearrange("(kc p) c -> p kc c", p=128)
        for j in (0, 1, 2):
            ws.add(f"in{l}_{j}", v[:, :, 256 * j:256 * j + 256], 128, 8, 256)
        for j in (13, 14, 15, 16):
            ws.add(f"in{l}_{j}", v[:, :, 256 * j:256 * j + 256], 128, 8, 256)
        ws.add(f"ws{l}_0", w_sT[l].rearrange("p (a b) -> p a b", a=8), 128, 8, 128)
        for j in (9, 10, 11, 12):
            ws.add(f"in{l}_{j}", v[:, :, 256 * j:256 * j + 256], 128, 8, 256)
        vb = w_pb[l].rearrange("(kc p) c -> p kc c", p=128)
        for mp in range(4):
            ws.add(f"in{l}_{17 + mp}", v[:, :, 256 * (17 + mp):256 * (18 + mp)], 128, 8, 256)
            ws.add(f"in{l}_{21 + mp}", v[:, :, 256 * (21 + mp):256 * (22 + mp)], 128, 8, 256)
            ws.add(f"pa{l}_{mp}", w_pa[l][:, :, 256 * mp:256 * mp + 256], 64, 4, 256)
            ws.add(f"pb{l}_{mp}", vb[:, :, 256 * mp:256 * mp + 256], 128, 8, 256)
        vo = w_o[l].rearrange("(kc p) c -> p kc c", p=128)
        for j in range(4):
            ws.add(f"wo{l}_{j}", vo[:, :, 256 * j:256 * j + 256], 128, 8, 256)

    def plan_layer_C(l):
        vu = w_up[l].rearrange("(kc p) c -> p kc c", p=128)
        for j in range(NPAIR):
            ws.add(f"up{l}_{j}", vu[:, :, 256 * j:256 * j + 256], 128, 8, 256)
        vd = w_dn[l].rearrange("(kc p) c -> p kc c", p=128)
        for m in range(8):
            for hf in range(2):
                ws.add(f"dn{l}_{m}_{hf}", vd[:, 11 * hf:11 * hf + 11, 128 * m:128 * m + 128], 128, 11, 128)

    for l in range(depth):
        for s in range(nseg):
            plan_layer_A(l)
        for s in range(nseg):
            plan_layer_B(l)
        for u in range(NU):
            plan_layer_C(l)

    ws.finalize()
    prev_bufs = []

    pass_cache = {}
    last_kind = [None]

    def new_pass(kind=None):
        W = Bump(WBASE, WLIM)
        if kind is not None and kind == last_kind[0]:
            cache = pass_cache[kind]

            def mkc(name):
                return cache[name]
            return W, mkc
        ev = S.fence(prev_bufs)
        prev_bufs.clear()
        cache = {}
        pass_cache[kind] = cache
        last_kind[0] = kind

        def mk(name):
            b = S.buf(name, init=ev)
            prev_bufs.append(b)
            cache[name] = b
            return b
        return W, mk

    def ck(name):
        if stop == name:
            S.emit()
            raise _Stop(nc)

    DMA("sp", ident_f, ident_in, (), [cb], "c0")
    DMA("sp", segmask, segmask_in, (), [cb], "c1")
    DMA("sp", convflag, convflag_in, (), [cb], "c2")
    DMA("sp", relb, relb_in.partition_broadcast(128), (), [cb], "c3")
    DMA("sp", nfg, nf_in, (), [cb], "c4")
    MEMSET("dve", ones_mean, 1.0 / 1024.0, [cb])
    MEMSET("dve", ones_bf, 1.0, [cb])
    MEMSET("dve", epscol, EPS, [cb])
    MEMSET("dve", zerocol, 0.0, [cb])

    W, mk = new_pass()
    Z = W([6144], BF16)
    Zb = mk("Z")
    MEMSET("pool", Z, 0.0, [Zb])
    for c in range(6):
        DMA("sp", KT[c * 128:(c + 1) * 128, 0:PAD], Z[:, 0:PAD], [Zb], [KTp], dk("z", 4))
        DMA("sp", KT[c * 128:(c + 1) * 128, PAD + T:PAD + T + PAD], Z[:, 0:PAD], [Zb], [KTp], dk("z", 4))
    Z3 = Z.rearrange("p (a c) -> p a c", a=8)
    DMA("sp", VS[0:PAD, :].rearrange("(a p) c -> p a c", p=128), Z3, [Zb], [VSp], dk("z", 4))
    DMA("sp", VS[PAD + T:PAD + T + PAD, :].rearrange("(a p) c -> p a c", p=128), Z3, [Zb], [VSp], dk("z", 4))
    Zf = Z[:, 0:16].bitcast(F32).rearrange("p (a o) -> p a o", o=1)
    for Xv, hb in ((XAv, XAh), (XBv, XBh)):
        DMA("sp", Xv[:, :, 0:1], Zf, [Zb], [hb], dk("z", 4), slow=True)
        DMA("sp", Xv[:, :, T + 1:T + 2], Zf, [Zb], [hb], dk("z", 4), slow=True)
    EP = [W([128], F32) for i in range(2)]
    EPb = [mk(f"ep{i}") for i in range(2)]
    ei = 0
    for g in range(3):
        for ty in range(2):
            lst = _PDESC[g * 2 + ty]
            for n_, (b, pi) in enumerate(lst):
                DMA("sp", EP[ei % 2], epl_in[pi], (), [EPb[ei % 2]], f"ep{ei % 2}")
                for h in range(4):
                    sc = NEG if b < 0 else relb[:, b * 12 + 4 * g + h:b * 12 + 4 * g + h + 1]
                    if n_ == 0:
                        TS("dve", TAB[g][:, 2 * h + ty, :], EP[ei % 2], sc, None, ALU.mult, None, [EPb[ei % 2], cb], [tabb])
                    else:
                        STT("dve", TAB[g][:, 2 * h + ty, :], EP[ei % 2], sc, TAB[g][:, 2 * h + ty, :], ALU.mult, ALU.add,
                            [EPb[ei % 2], cb, tabb], [tabb])
                ei += 1

    ck("init")
    def rmsnorm(xt, xb, n, gain, gb, out, ob, SQ, SQb, RS, RSb):
        ACT(SQ[:, :, 0:n], xt, AF.Square, [xb], [SQb])
        bk, bb = pb()
        for kc in range(8):
            MM(bk[:, 0:n], ones_mean, SQ[:, kc, 0:n], kc == 0, kc == 7, [SQb, cb], [bb])
        ACT(RS[:, 0:n], bk[:, 0:n], AF.Sqrt, [bb, cb], [RSb], bias=epscol)
        RCP(RS[:, 0:n], RS[:, 0:n], [RSb], [RSb])
        for kc in range(8):
            STT("dve", out[:, kc, :], xt[:, kc, :], gain[:, kc:kc + 1], RS[:, 0:n],
                ALU.mult, ALU.mult, [xb, RSb, gb], [ob])

    XIN = [W([1024], F32) for i in range(2)]
    XINb = [mk(f"xin{i}") for i in range(2)]
    XT0 = [W([8, 128], F32) for i in range(2)]
    XT0b = [mk(f"xt0{i}") for i in range(2)]
    for tt in range(T // 128):
        i2 = tt % 2
        DMA("sp", XIN[i2], x_in[tt * 128:(tt + 1) * 128, :], (), [XINb[i2]], f"xin{i2}")
        for half in range(2):
            bk, bb = pb()
            for q in range(4):
                kc = half * 4 + q
                TR(bk[:, q * 128:(q + 1) * 128], XIN[i2][:, kc * 128:(kc + 1) * 128], [XINb[i2]], [bb])
            EVAC(XT0[i2][:, half * 4:half * 4 + 4, :], bk.rearrange("p (a b) -> p a b", a=4), [bb], [XT0b[i2]])
        DMA(STQ, XAv[:, :, 1 + tt * 128:1 + (tt + 1) * 128], XT0[i2], [XT0b[i2]], [XAb[tt // 4]], dk("st", 4))

    ck("p0")
    for l in range(depth):
        DMA("sp", pp, pp_in[l], (), [ppb], "pp0")
        DMA("sp", vg_bc, vg_in[l].partition_broadcast(128), (), [ppb], "pp1")
        DMA("sp", bs_bc.rearrange("p a b -> p (a b)"), bs_in[l].partition_broadcast(128), (), [ppb], "pp2")

        for s in range(nseg):
            W, mk = new_pass("A")
            XS = [W([8, 512], F32) for i in range(2)]
            XSb = [mk(f"xs{i}") for i in range(2)]
            SQ = W([8, 512], BF16)
            SQb = mk("sq")
            RS = W([512], F32)
            RSb = mk("rs")
            HT2 = [W([8, SEG], BF16) for i in range(2)]
            HT2b = [mk(f"hts{i}") for i in range(2)]
            HTs, HTsb = HT2[s % 2], HT2b[s % 2]
            KST = [W([SEG], BF16) for i in range(2)]
            KSTb = [mk(f"kst{i}") for i in range(2)]
            VST = [W([16, 256], BF16) for i in range(2)]
            VSTb = [mk(f"vst{i}") for i in range(2)]
            def a_norm(s_):
                H_, Hb_ = HT2[s_ % 2], HT2b[s_ % 2]
                for t in range(4):
                    gt = s_ * 4 + t
                    DMA("sp", XS[t % 2], XAv[:, :, 1 + gt * 512:1 + (gt + 1) * 512], [XAb[gt]], [XSb[t % 2]], f"xs{t % 2}")
                    rmsnorm(XS[t % 2], XSb[t % 2], 512, pp[:, 0:8], ppb, H_[:, :, t * 512:(t + 1) * 512], Hb_, SQ, SQb, RS, RSb)
                DMA(STQ, HTv[:, :, s_ * SEG:(s_ + 1) * SEG], H_, [Hb_], [HTb[s_]], dk("st", 4))

            if s == 0:
                a_norm(0)
            ki = 0
            for j in (3, 4, 5):
                ci, wv, wb_ = wget(f"in{l}_{j}")
                g = j - 3
                for m in range(2):
                    for t in range(4):
                        bk, bb = pb()
                        for kc in range(8):
                            MM(bk, wv[:, kc, m * 128:(m + 1) * 128], HTs[:, kc, t * 512:(t + 1) * 512], kc == 0, kc == 7, [wb_, HTsb], [bb])
                        EVAC(KST[ki % 2][:, t * 512:(t + 1) * 512], bk, [bb], [KSTb[ki % 2]])
                    r0 = g * 256 + m * 128
                    DMA(STQ, KT[r0:r0 + 128, PAD + s * SEG:PAD + (s + 1) * SEG], KST[ki % 2], [KSTb[ki % 2]], [KTb[s]], dk("st", 4))
                    ki += 1
                ws.release(ci)
            for j in (6, 7, 8):
                if j == 7 and s + 1 < nseg:
                    a_norm(s + 1)
                ci, wv, wb_ = wget(f"in{l}_{j}")
                g = j - 6
                vi = j % 2
                for t2 in range(8):
                    bk, bb = pb()
                    for q in range(2):
                        tt = t2 * 2 + q
                        for kc in range(8):
                            MM(bk[:, q * 256:(q + 1) * 256], HTs[:, kc, tt * 128:(tt + 1) * 128], wv[:, kc, :], kc == 0, kc == 7, [wb_, HTsb], [bb])
                    EVAC(VST[vi][:, t2 * 2:t2 * 2 + 2, :], bk.rearrange("p (a b) -> p a b", a=2), [bb], [VSTb[vi]])
                DMA(STQ, VS[PAD + s * SEG:PAD + (s + 1) * SEG, g * 256:(g + 1) * 256].rearrange("(tt p) c -> p tt c", p=128),
                    VST[vi], [VSTb[vi]], [VSb[s]], dk("st", 4))
                ws.release(ci)

        ck("A")
        for s in range(nseg):
            W, mk = new_pass(None)
            HTs = W([8, SEG], BF16)
            HTsb = mk("hts")
            OT = W([4, SEG], BF16)
            OTb = mk("ot")
            RA = W.o
            QT = W([6, SEG], BF16)
            QTb = mk("qt")
            KW = [W([4096], BF16) for i in range(2)]
            KWb = [mk(f"kw{i}") for i in range(2)]
            VW = [W([32, 128], BF16) for i in range(2)]
            VWb = [mk(f"vw{i}") for i in range(2)]
            ACN = W([2, SEG], F32)
            ACNb = mk("acn")
            ACD = W([2, SEG], F32)
            ACDb = mk("acd")
            TMP = [W([512], F32) for i in range(3)]
            TMPb = [mk(f"tmp{i}") for i in range(3)]
            PT = [W([1024], BF16) for i in range(2)]
            PTb = [mk(f"pt{i}") for i in range(2)]
            RAend = W.o

            DMA("sp", HTs, HTv[:, :, s * SEG:(s + 1) * SEG], [HTb[s]], [HTsb], "hts")
            for j in (0, 1, 2):
                ci, wv, wb_ = wget(f"in{l}_{j}")
                for m in range(2):
                    for t in range(4):
                        bk, bb = pb()
                        for kc in range(8):
                            MM(bk, wv[:, kc, m * 128:(m + 1) * 128], HTs[:, kc, t * 512:(t + 1) * 512], kc == 0, kc == 7, [wb_, HTsb], [bb])
                        EVAC(QT[:, 2 * j + m, t * 512:(t + 1) * 512], bk, [bb], [QTb])
                ws.release(ci)
            ck("B1q")
            kvi = 0
            tpi = 0
            for pair in range(2):
                for g in range(3):
                    d = DIL[g]
                    nqt = 16 // d
                    Wn = SEG + 128 * d
                    kv = kvi % 2
                    kvi += 1
                    r0 = g * 256 + pair * 128
                    c0 = PAD + s * SEG - 64 * d
                    kdeps = [KTb[s], KTp] + ([KTb[s - 1]] if s > 0 else []) + ([KTb[s + 1]] if s + 1 < nseg else [])
                    vdeps = [VSb[s], VSp] + ([VSb[s - 1]] if s > 0 else []) + ([VSb[s + 1]] if s + 1 < nseg else [])
                    DMA("sp", KW[kv][:, 0:Wn], KT[r0:r0 + 128, c0:c0 + Wn], kdeps, [KWb[kv]], f"kw{kv}")
                    for r in range(d):
                        src = VS[c0 + r:c0 + r + ((nqt + 1) * 128 - 1) * d + 1:d, r0:r0 + 128].rearrange("(j p) c -> p j c", p=128)
                        DMA("sp", VW[kv][:, r * (nqt + 1):(r + 1) * (nqt + 1), :], src, vdeps, [VWb[kv]], dk("vw", 4))
                    ck("B1l")
                    if d < 16:
                        groups = [[(r, qt), (r, qt + 1)] for r in range(d) for qt in range(0, nqt, 2)]
                    else:
                        groups = [[(r, 0), (r + 1, 0)] for r in range(0, d, 2)]
                    for grp in groups:
                        pi_ = tpi % 2
                        bks = [pb(), pb()]
                        for h2 in range(2):
                            bk, bb = bks[h2]
                            for qi, (r, qt) in enumerate(grp):
                                q0 = r + qt * 128 * d
                                qsl = slice(q0, q0 + 127 * d + 1, d)
                                for kt in range(2):
                                    k0 = (qt + kt) * 128 * d + r
                                    ksl = slice(k0, k0 + 127 * d + 1, d)
                                    cc0 = qi * 256 + kt * 128
                                    MM(bk[:, cc0:cc0 + 128], KW[kv][h2 * 64:(h2 + 1) * 64, ksl],
                                       QT[h2 * 64:(h2 + 1) * 64, 2 * g + pair, qsl], True, True, [KWb[kv], QTb], [bb])
                        PTv = PT[pi_].rearrange("p (h q k n) -> p h q k n", h=2, q=2, k=2)
                        for h2 in range(2):
                            bk, bb = bks[h2]
                            ti = (tpi * 2 + h2) % 3
                            hh = pair * 2 + h2
                            TMv = TMP[ti].rearrange("p (q k n) -> p q k n", q=2, k=2)
                            STT("dve", TMv, bk.rearrange("p (q k n) -> p q k n", q=2, k=2), 0.125,
                                TAB[g][:, 2 * hh:2 * hh + 2, :].unsqueeze(1).broadcast_to([128, 2, 2, 128]), ALU.mult, ALU.add,
                                [bb, tabb], [TMPb[ti]])
                            masked = []
                            for qi, (r, qt) in enumerate(grp):
                                if qt == 0:
                                    masked.append((qi, 0, segmask[:, 2 * s:2 * s + 1]))
                                if qt == nqt - 1:
                                    masked.append((qi, 1, segmask[:, 2 * s + 1:2 * s + 2]))
                            if len(masked) == 4:
                                for kt in range(2):
                                    bcol = segmask[:, 2 * s + kt:2 * s + kt + 1]
                                    ACT(PTv[:, h2, :, kt, :], TMv[:, :, kt, :], AF.Exp, [TMPb[ti], cb], [PTb[pi_]], bias=bcol)
                            else:
                                ACT(PTv[:, h2].rearrange("p q k n -> p (q k n)"), TMP[ti], AF.Exp, [TMPb[ti], cb], [PTb[pi_]], bias=zerocol)
                                for (qi, kt, bcol) in masked:
                                    ACT(PTv[:, h2, qi, kt, :], TMv[:, qi, kt, :], AF.Exp, [TMPb[ti], cb], [PTb[pi_]], bias=bcol)
                        tpi += 1
                        bkn, bbn = pb()
                        bkd, bbd = pb()
                        for h2 in range(2):
                            for qi, (r, qt) in enumerate(grp):
                                cc0 = h2 * 256 + qi * 128
                                for kt in range(2):
                                    vt = r * (nqt + 1) + qt + kt
                                    MM(bkn[0:64, cc0:cc0 + 128], VW[kv][:, vt, h2 * 64:(h2 + 1) * 64], PTv[:, h2, qi, kt, :],
                                       kt == 0, kt == 1, [VWb[kv], PTb[pi_]], [bbn])
                        for h2 in range(2):
                            for qi, (r, qt) in enumerate(grp):
                                cc0 = h2 * 256 + qi * 128
                                for kt in range(2):
                                    MM(bkd[0:64, cc0:cc0 + 128], ones_bf[:, 0:64], PTv[:, h2, qi, kt, :],
                                       kt == 0, kt == 1, [cb, PTb[pi_]], [bbd])
                        if d < 16:
                            r, qt = grp[0]
                            q0 = r + qt * 128 * d
                            dsl = slice(q0, q0 + 255 * d + 1, d)
                            an = ACN[0:64, :, dsl]
                            ad = ACD[0:64, :, dsl]
                            nv = bkn[0:64, :].rearrange("p (a b) -> p a b", a=2)
                            dv = bkd[0:64, :].rearrange("p (a b) -> p a b", a=2)
                        else:
                            r = grp[0][0]
                            an = ACN[0:64].rearrange("p h (l q) -> p h q l", q=16)[:, :, r:r + 2, :]
                            ad = ACD[0:64].rearrange("p h (l q) -> p h q l", q=16)[:, :, r:r + 2, :]
                            nv = bkn[0:64, :].rearrange("p (a q l) -> p a q l", a=2, q=2)
                            dv = bkd[0:64, :].rearrange("p (a q l) -> p a q l", a=2, q=2)
                        if g == 0:
                            CP("act", an, nv, [bbn], [ACNb])
                            CP("act", ad, dv, [bbd], [ACDb])
                        else:
                            TT("dve", an, an, nv, ALU.add, [bbn, ACNb], [ACNb])
                            TT("dve", ad, ad, dv, ALU.add, [bbd, ACDb], [ACDb])
                    ck("B1g")
                for t in range(4):
                    sl = slice(t * 512, (t + 1) * 512)
                    RCP(ACD[0:64, :, sl], ACD[0:64, :, sl], [ACDb], [ACDb])
                    TT("dve", OT[0:64, pair * 2:pair * 2 + 2, sl], ACN[0:64, :, sl], ACD[0:64, :, sl], ALU.mult, [ACNb, ACDb], [OTb])

            ck("B1")
            ev2 = S.fence([QTb, ACNb, ACDb] + KWb + VWb + TMPb + PTb)
            W2 = Bump(RA, WLIM)

            def mk2(name):
                b = S.buf(name, init=ev2)
                prev_bufs.append(b)
                return b
            VN = W2([16, 1024], BF16)
            VNb = mk2("vn")
            SG = W2([8, SEG], BF16)
            SGb = mk2("sg")
            GV = [W2([1024], F32) for i in range(2)]
            GVb = [mk2(f"gv{i}") for i in range(2)]
            SQV = W2([1024], BF16)
            SQVb = mk2("sqv")
            SS = [W2([1], F32) for i in range(2)]
            SSb = [mk2(f"ss{i}") for i in range(2)]
            UT = [W2([512], BF16) for i in range(2)]
            UTb = [mk2(f"ut{i}") for i in range(2)]
            ZT = [W2([512], F32) for i in range(2)]
            ZTb = [mk2(f"zt{i}") for i in range(2)]
            SGA = [W2([512], F32) for i in range(2)]
            SGAb = [mk2(f"sga{i}") for i in range(2)]
            SGB = [W2([512], F32) for i in range(2)]
            SGBb = [mk2(f"sgb{i}") for i in range(2)]
            T1 = [W2([512], F32) for i in range(2)]
            T1b = [mk2(f"t1{i}") for i in range(2)]
            T2 = [W2([512], F32) for i in range(2)]
            T2b = [mk2(f"t2{i}") for i in range(2)]
            vch = [wget(f"in{l}_{j}") for j in (13, 14, 15, 16)]
            for tt in range(16):
                i2 = tt % 2
                bks = [pb(), pb()]
                for c4 in range(4):
                    ci, wv, wb_ = vch[c4]
                    bk, bb = bks[c4 // 2]
                    for kc in range(8):
                        MM(bk[:, (c4 % 2) * 256:(c4 % 2 + 1) * 256], HTs[:, kc, tt * 128:(tt + 1) * 128], wv[:, kc, :], kc == 0, kc == 7,
                           [wb_, HTsb], [bb])
                for hb in range(2):
                    ACT(GV[i2][:, hb * 512:(hb + 1) * 512], bks[hb][0], AF.Gelu_apprx_tanh, [bks[hb][1]], [GVb[i2]])
                ACT(SQV, GV[i2], AF.Square, [GVb[i2]], [SQVb])
                S.op("dve", lambda e, o=SS[i2], i=SQV: e.reduce_sum(out=o, in_=i, axis=AX.X), [SQVb], [SSb[i2]], nowaw=True)
                ACT(SS[i2], SS[i2], AF.Sqrt, [SSb[i2], cb], [SSb[i2]], bias=epscol, scale=1.0 / 1024.0)
                RCP(SS[i2], SS[i2], [SSb[i2]], [SSb[i2]])
                STT("dve", VN[:, tt, :], GV[i2], SS[i2], vg_bc, ALU.mult, ALU.mult, [GVb[i2], SSb[i2], ppb], [VNb])
            for ci, wv, wb_ in vch:
                ws.release(ci)
            cws, wsv, wsb = wget(f"ws{l}_0")
            ui = 0
            for gp in range(4):
                ci, wv, wb_ = wget(f"in{l}_{9 + gp}")
                for gg in range(2):
                    g8 = gp * 2 + gg
                    for t in range(4):
                        sl = slice(t * 512, (t + 1) * 512)
                        i2 = ui % 2
                        ui += 1
                        bk, bb = pb()
                        for kc in range(8):
                            MM(bk, wv[:, kc, gg * 128:(gg + 1) * 128], HTs[:, kc, sl], kc == 0, kc == 7, [wb_, HTsb], [bb])
                        ACT(UT[i2], bk, AF.Gelu_apprx_tanh, [bb], [UTb[i2]])
                        bz, bzb = pb()
                        for n4 in range(4):
                            MM(bz[:, n4 * 128:(n4 + 1) * 128], VN[:, t * 4 + n4, g8 * 128:(g8 + 1) * 128], wsv[:, g8, :], True, True,
                               [VNb, wsb], [bzb])
                        TT("dve", ZT[i2].rearrange("p (a b) -> p a b", a=4), bz.rearrange("p (a b) -> p a b", a=4),
                           bs_bc[:, g8, :].unsqueeze(1).broadcast_to([128, 4, 128]), ALU.add, [bzb, ppb], [ZTb[i2]])
                        TT("pool", SG[:, g8, sl], ZT[i2], UT[i2], ALU.mult, [ZTb[i2], UTb[i2]], [SGb])
                ws.release(ci)
            ws.release(cws)
            MG = VN.rearrange("p a b -> p (a b)").rearrange("p (a b) -> p a b", a=8)
            MGb = S.buf("mg", init=S.fence([VNb]))
            prev_bufs.append(MGb)
            mi = 0
            for mp in range(4):
                cga, wga, bga = wget(f"in{l}_{17 + mp}")
                cgb, wgb, bgb = wget(f"in{l}_{21 + mp}")
                cpa, wpa, bpa = wget(f"pa{l}_{mp}")
                cpb, wpb, bpb = wget(f"pb{l}_{mp}")
                for mm in range(2):
                    m = mp * 2 + mm
                    ms = slice(mm * 128, (mm + 1) * 128)
                    for t in range(4):
                        sl = slice(t * 512, (t + 1) * 512)
                        i2 = mi % 2
                        mi += 1
                        bk, bb = pb()
                        for kc in range(8):
                            MM(bk, wga[:, kc, ms], HTs[:, kc, sl], kc == 0, kc == 7, [bga, HTsb], [bb])
                        ACT(SGA[i2], bk, AF.Sigmoid, [bb], [SGAb[i2]])
                        bk, bb = pb()
                        for kc in range(8):
                            MM(bk, wgb[:, kc, ms], HTs[:, kc, sl], kc == 0, kc == 7, [bgb, HTsb], [bb])
                        ACT(SGB[i2], bk, AF.Sigmoid, [bb], [SGBb[i2]])
                        bk, bb = pb()
                        for h in range(4):
                            MM(bk, wpa[0:64, h, ms], OT[0:64, h, sl], h == 0, h == 3, [bpa, OTb], [bb])
                        TT("dve", T1[i2], bk, SGA[i2], ALU.mult, [bb, SGAb[i2]], [T1b[i2]])
                        bk, bb = pb()
                        for kc in range(8):
                            MM(bk, wpb[:, kc, ms], SG[:, kc, sl], kc == 0, kc == 7, [bpb, SGb], [bb])
                        TT("dve", T2[i2], bk, SGB[i2], ALU.mult, [bb, SGBb[i2]], [T2b[i2]])
                        TT("pool", MG[:, m, sl], T1[i2], T2[i2], ALU.add, [T1b[i2], T2b[i2]], [MGb])
                for c_ in (cga, cgb, cpa, cpb):
                    ws.release(c_)
            evx = S.fence([HTsb])
            XS = [HTs.rearrange("p a b -> p (a b)")[:, i * 8192:(i + 1) * 8192].bitcast(F32).rearrange("p (a b) -> p a b", a=8) for i in range(2)]
            XSb = []
            for i in range(2):
                b_ = S.buf(f"xsB{i}", init=evx)
                prev_bufs.append(b_)
                XSb.append(b_)
            och = [wget(f"wo{l}_{j}") for j in range(4)]
            for t in range(4):
                gt = s * 4 + t
                i2 = t % 2
                sl = slice(t * 512, (t + 1) * 512)
                DMA("sp", XS[i2], XAv[:, :, 1 + gt * 512:1 + (gt + 1) * 512], [XAb[gt]], [XSb[i2]], f"xsB{i2}")
                for m in range(8):
                    ci, wv, wb_ = och[m // 2]
                    bk, bb = pb()
                    for kc in range(8):
                        MM(bk, wv[:, kc, (m % 2) * 128:(m % 2 + 1) * 128], MG[:, kc, sl], kc == 0, kc == 7, [wb_, MGb], [bb])
                    TT("dve", XS[i2][:, m, :], bk, XS[i2][:, m, :], ALU.add, [bb, XSb[i2]], [XSb[i2]])
                DMA(STQ, XBv[:, :, 1 + gt * 512:1 + (gt + 1) * 512], XS[i2], [XSb[i2]], [XBb[gt]], dk("st", 4))
            for ci, wv, wb_ in och:
                ws.release(ci)

        ck("B")
        last = (l == depth - 1)
        for u in range(NU):
            W, mk = new_pass("C")
            XU = [W([8, 512], F32) for i in range(2)]
            XUb = [mk(f"xu{i}") for i in range(2)]
            XH = W([8, 2], F32)
            XHb = mk("xh")
            SQ = W([8, 512], BF16)
            SQb = mk("sq")
            RS = W([512], F32)
            RSb = mk("rs")
            H2 = W([8, 1026], BF16)
            H2b = mk("h2")
            GOFF = W.o
            G = W([NPAIR, UNIT], BF16)
            Gb = mk("g")
            RC = W.o
            AG = [W([1026], F32) for i in range(2)]
            AGb = [mk(f"ag{i}") for i in range(2)]
            AV = [W([1026], F32) for i in range(2)]
            AVb = [mk(f"av{i}") for i in range(2)]
            CG = [W([UNIT], F32) for i in range(2)]
            CGb = [mk(f"cg{i}") for i in range(2)]
            CV = [W([UNIT], F32) for i in range(2)]
            CVb = [mk(f"cv{i}") for i in range(2)]
            HH = W([8, 2], F32)
            HHb = mk("hh")

            def c_load(u_):
                c0 = u_ * UNIT
                for t in range(2):
                    gt = u_ * 2 + t
                    DMA("sp", XU[t], XBv[:, :, 1 + gt * 512:1 + (gt + 1) * 512], [XBb[gt]], [XUb[t]], f"xu{t}")
                hdeps = [XBh] + ([XBb[u_ * 2 - 1]] if u_ > 0 else []) + ([XBb[u_ * 2 + 2]] if u_ * 2 + 2 < T // 512 else [])
                DMA("sp", XH[:, :, 0:1], XBv[:, :, c0:c0 + 1], hdeps, [XHb], "xh0", slow=True)
                DMA("sp", XH[:, :, 1:2], XBv[:, :, c0 + UNIT + 1:c0 + UNIT + 2], hdeps, [XHb], "xh1", slow=True)

            def c_norm(u_):
                for t in range(2):
                    rmsnorm(XU[t], XUb[t], 512, pp[:, 8:16], ppb, H2[:, :, 1 + t * 512:1 + (t + 1) * 512], H2b, SQ, SQb, RS, RSb)
                rmsnorm(XH, XHb, 2, pp[:, 8:16], ppb, HH, HHb, SQ, SQb, RS, RSb)
                for side in range(2):
                    col = 0 if side == 0 else 1025
                    TS("dve", H2[:, :, col:col + 1], HH[:, :, side:side + 1], convflag[:, 2 * u_ + side:2 * u_ + side + 1], None, ALU.mult, None,
                       [HHb, cb], [H2b])

            if u == 0:
                c_load(0)
                c_norm(0)
            pieces = ((0, 512), (512, 512), (1024, 2))
            for j in range(NPAIR):
                if j == 1 and u + 1 < NU:
                    c_load(u + 1)
                ci, wv, wb_ = wget(f"up{l}_{j}")
                i2 = j % 2
                for m in range(2):
                    A_, Ab_ = (AG[i2], AGb[i2]) if m == 0 else (AV[i2], AVb[i2])
                    for (o0, n) in pieces:
                        bk, bb = pb()
                        for kc in range(8):
                            MM(bk[:, 0:n], wv[:, kc, m * 128:(m + 1) * 128], H2[:, kc, o0:o0 + n], kc == 0, kc == 7, [wb_, H2b], [bb])
                        CP("act", A_[:, o0:o0 + n], bk[:, 0:n], [bb], [Ab_])
                        C_, Cb_ = (CG[i2], CGb[i2]) if m == 0 else (CV[i2], CVb[i2])
                        cc = 2 * j + m
                        lo = 1 if o0 == 0 else 0
                        hi = min(o0 + n, 1025) - o0
                        S.op("act", lambda e, o=C_[:, o0 + lo - 1:o0 + hi - 1], i=bk[:, lo:hi], sc=pp[:, 60 + cc:61 + cc], bi=pp[:, 148 + cc:149 + cc]:
                             e.activation(out=o, in_=i, func=AF.Identity, bias=bi, scale=sc), [bb, ppb], [Cb_], nowaw=True)
                ws.release(ci)
                for m in range(2):
                    A_, Ab_ = (AG[i2], AGb[i2]) if m == 0 else (AV[i2], AVb[i2])
                    C_, Cb_ = (CG[i2], CGb[i2]) if m == 0 else (CV[i2], CVb[i2])
                    eng = "dve"
                    cc = 2 * j + m
                    w0 = pp[:, 16 + cc:17 + cc]
                    w1 = pp[:, 60 + cc:61 + cc]
                    w2 = pp[:, 104 + cc:105 + cc]
                    bcv = pp[:, 148 + cc:149 + cc]
                    STT(eng, C_, A_[:, 0:1024], w0, C_, ALU.mult, ALU.add, [Ab_, ppb, Cb_], [Cb_])
                    STT(eng, C_, A_[:, 2:1026], w2, C_, ALU.mult, ALU.add, [Ab_, ppb, Cb_], [Cb_])
                ACT(CG[i2], CG[i2], AF.Gelu_apprx_tanh, [CGb[i2]], [CGb[i2]])
                TT("dve", G[:, j, :], CG[i2], CV[i2], ALU.mult, [CGb[i2], CVb[i2]], [Gb])
            rcb = AGb + AVb + CGb + CVb
            evr = S.fence(rcb)
            WR = Bump(RC, WLIM)
            XR = [WR([8, 512], F32) for i in range(2)]
            XRb = [S.buf(f"xr{i}", init=evr) for i in range(2)]
            for t in range(2):
                gt = u * 2 + t
                DMA("sp", XR[t], XBv[:, :, 1 + gt * 512:1 + (gt + 1) * 512], [XBb[gt]], [XRb[t]], f"xr{t}")
            for m in range(8):
                if m == 4 and u + 1 < NU:
                    c_norm(u + 1)
                bks = [pb(), pb()]
                for hf in range(2):
                    ci, wv, wb_ = wget(f"dn{l}_{m}_{hf}")
                    for t in range(2):
                        bk, bb = bks[t]
                        for k11 in range(11):
                            kc = hf * 11 + k11
                            MM(bk, wv[:, k11, :], G[:, kc, t * 512:(t + 1) * 512], kc == 0, kc == 21, [wb_, Gb], [bb])
                    ws.release(ci)
                for t in range(2):
                    TT("dve", XR[t][:, m, :], bks[t][0], XR[t][:, m, :], ALU.add, [bks[t][1], XRb[t]], [XRb[t]])
            if not last:
                for t in range(2):
                    gt = u * 2 + t
                    DMA(STQ, XAv[:, :, 1 + gt * 512:1 + (gt + 1) * 512], XR[t], [XRb[t]], [XAb[gt]], dk("st", 4))
                S.absorb(rcb, XRb)
            else:
                evy = S.fence([Gb])
                WY = Bump(GOFF, WLIM)
                YT = WY([8, 512], F32)
                YTb = S.buf("yt", init=evy)
                YO = [WY([1024], F32) for i in range(2)]
                YOb = [S.buf(f"yo{i}", init=evy) for i in range(2)]
                for t in range(2):
                    gt = u * 2 + t
                    rmsnorm(XR[t], XRb[t], 512, nfg, cb, YT, YTb, SQ, SQb, RS, RSb)
                    for q in range(4):
                        tt = gt * 4 + q
                        i2 = q % 2
                        for half in range(2):
                            bk, bb = pb()
                            for c4 in range(4):
                                kc = half * 4 + c4
                                TR(bk[:, c4 * 128:(c4 + 1) * 128], YT[:, kc, q * 128:(q + 1) * 128], [YTb], [bb])
                            EVAC(YO[i2][:, half * 512:(half + 1) * 512], bk, [bb], [YOb[i2]])
                        DMA(STQ, y_out[tt * 128:(tt + 1) * 128, :], YO[i2], [YOb[i2]], (), dk("yst", 4))
                S.absorb(rcb, XRb)
                S.absorb([Gb], [YTb] + YOb)

    assert wi[0] == len(ws.plan), (wi[0], len(ws.plan))
    S.emit()
    return nc


NSEG_FULL = 6
DEPTH_FULL = 4


def _core_layout():
    cores = []
    for c in range(8):
        segs = []
        if c < 4:
            for k in range(4):
                segs.append(("p", c, k * SEG))
            segs.append(("s", 2 * c, 0))
            segs.append(("s", 2 * c + 1, 0))
        else:
            for k in range(6):
                segs.append(("s", 8 + 6 * (c - 4) + k, 0))
        cores.append(segs)
    return cores


def _flags(segs):
    n = len(segs)
    soft_start = []
    for i, sg in enumerate(segs):
        if i > 0 and sg[0] == "p" and segs[i - 1][0] == "p" and segs[i - 1][1] == sg[1] and segs[i - 1][2] + SEG == sg[2]:
            soft_start.append(True)
        else:
            soft_start.append(False)
    soft_end = [(i + 1 < n and soft_start[i + 1]) for i in range(n)]
    segmask = np.zeros((128, 2 * n), np.float32)
    nu = n * (SEG // UNIT)
    convflag = np.zeros((128, 2 * nu), np.float32)
    for i in range(n):
        if not soft_start[i]:
            segmask[0:64, 2 * i] = NEG
        if not soft_end[i]:
            segmask[64:128, 2 * i + 1] = NEG
        for uu in range(SEG // UNIT):
            u = i * (SEG // UNIT) + uu
            st = soft_start[i] if uu == 0 else True
            en = soft_end[i] if uu == SEG // UNIT - 1 else True
            convflag[:, 2 * u] = 1.0 if st else 0.0
            convflag[:, 2 * u + 1] = 1.0 if en else 0.0
    return segmask, convflag


def _prep_shared(depth, rel_bias, norm_mix, w_in, v_gain, w_s, b_s, w_proj_a, w_proj_b, w_out,
                 norm_ffn, w_up, conv_w, conv_b, w_down, norm_final):
    f = lambda a: np.ascontiguousarray(np.asarray(a, dtype=np.float32))
    L = depth
    sh = {}
    sh["relb"] = f(rel_bias).reshape(1, 384)
    sh["epl"] = _PLANES
    sh["ident"] = np.eye(128, dtype=np.float32)
    sh["w_in"] = f(w_in[:L])
    sh["w_sT"] = f(np.transpose(np.asarray(w_s[:L]), (0, 3, 1, 2)).reshape(L, 128, 1024))
    sh["w_pa"] = f(np.asarray(w_proj_a[:L]).reshape(L, 4, 64, 1024).transpose(0, 2, 1, 3))
    sh["w_pb"] = f(w_proj_b[:L])
    sh["w_o"] = f(w_out[:L])
    wu = np.asarray(w_up[:L]).reshape(L, D, 2, NPAIR, 128).transpose(0, 1, 3, 2, 4).reshape(L, D, 2 * DFF)
    sh["w_up"] = f(wu)
    sh["w_dn"] = f(w_down[:L])
    pp = np.zeros((L, 128, 192), np.float32)
    pp[:, :, 0:8] = np.asarray(norm_mix[:L]).reshape(L, 8, 128).transpose(0, 2, 1)
    pp[:, :, 8:16] = np.asarray(norm_ffn[:L]).reshape(L, 8, 128).transpose(0, 2, 1)
    cw = np.asarray(conv_w[:L]).reshape(L, 3, 2, NPAIR, 128).transpose(0, 1, 4, 3, 2).reshape(L, 3, 128, 44)
    for tap in range(3):
        pp[:, :, 16 + 44 * tap:16 + 44 * (tap + 1)] = cw[:, tap]
    cbv = np.asarray(conv_b[:L]).reshape(L, 2, NPAIR, 128).transpose(0, 3, 2, 1).reshape(L, 128, 44)
    pp[:, :, 148:192] = cbv
    sh["pp"] = pp
    sh["nf"] = f(np.asarray(norm_final).reshape(8, 128).T)
    sh["vg"] = f(np.asarray(v_gain[:L]).reshape(L, 1, 1024))
    sh["bs"] = f(np.asarray(b_s[:L]).reshape(L, 1, 1024))
    return sh


_NC_CACHE = {}


def kernel(x_prompt, x_sample, rel_bias, norm_mix, w_in, v_gain, w_s, b_s, w_proj_a, w_proj_b,
           w_out, norm_ffn, w_up, conv_w, conv_b, w_down, norm_final):
    xp = np.asarray(x_prompt, dtype=np.float32)
    xs = np.asarray(x_sample, dtype=np.float32)
    sh = _prep_shared(DEPTH_FULL, rel_bias, norm_mix, w_in, v_gain, w_s, b_s, w_proj_a, w_proj_b, w_out,
                      norm_ffn, w_up, conv_w, conv_b, w_down, norm_final)
    cores = _core_layout()
    in_maps = []
    for segs in cores:
        xc = np.concatenate([(xp if w == "p" else xs)[b, st:st + SEG] for (w, b, st) in segs], axis=0)
        sm, cf = _flags(segs)
        m = dict(sh)
        m["x"] = np.ascontiguousarray(xc)
        m["segmask"] = sm
        m["convflag"] = cf
        in_maps.append(m)
    key = (NSEG_FULL, DEPTH_FULL)
    if key not in _NC_CACHE:
        _NC_CACHE[key] = build(*key)
    nc = _NC_CACHE[key]
    res = run_bass_kernel_spmd(nc, in_maps, core_ids=list(range(8)))
    yp = np.zeros_like(xp)
    ys = np.zeros_like(xs)
    for c, segs in enumerate(cores):
        yc = res.results[c]["y"]
        for i, (w, b, st) in enumerate(segs):
            (yp if w == "p" else ys)[b, st:st + SEG] = yc[i * SEG:(i + 1) * SEG]
    return (yp, ys)
```
